# Optimizing a Trainium2 kernel written in Bass

```python
import jax, jax.numpy as jnp
from jax import lax
import numpy as np

D_MODEL = 2048
BATCH = 4
SEQ = 2048
DEPTH = 2
DEC_BATCH = 8
DEC_SEQ = 32
PAST_LEN = 2048

CHUNK = 64
MIX_WIDTH = D_MODEL
GLA_WIDTH = MIX_WIDTH // 2
FOX_WIDTH = MIX_WIDTH - GLA_WIDTH
GLA_HEADS = 4
GLA_DV = GLA_WIDTH // GLA_HEADS
GLA_DK = GLA_DV // 2
GLA_RANK = 16
GLA_TAU = 16.0
FOX_HD = 128
FOX_HEADS = FOX_WIDTH // FOX_HD
FOX_QBLK = 128
D_FF = 4 * D_MODEL
EPS = 1e-6
PROJ_SIZES = (GLA_HEADS * GLA_DK, GLA_HEADS * GLA_DK, GLA_WIDTH, GLA_RANK, GLA_WIDTH,
              FOX_WIDTH, FOX_WIDTH, FOX_WIDTH, FOX_HEADS)
PROJ_WIDTH = 2 * GLA_HEADS * GLA_DK + 2 * GLA_WIDTH + GLA_RANK + 3 * FOX_WIDTH + FOX_HEADS

kernel_name = "hymba_gla_fox_stream_step"


def rmsnorm(x, g):
    xf = x.astype(jnp.float32)
    y = xf * lax.rsqrt(jnp.mean(xf * xf, axis=-1, keepdims=True) + EPS)
    return (y * g.astype(jnp.float32)).astype(x.dtype)


def project(h, w_in, w_gate_up, b_gate, b_fox_f):
    B, L, _ = h.shape
    z = jnp.einsum('bld,de->ble', h, w_in)
    idx = np.cumsum(PROJ_SIZES)[:-1].tolist()
    qg, kg, vg, glr, rg, qf, kf, vf, fl = jnp.split(z, idx, axis=-1)
    qg = qg.reshape(B, L, GLA_HEADS, GLA_DK) * (GLA_DK ** -0.5)
    kg = kg.reshape(B, L, GLA_HEADS, GLA_DK)
    vg = vg.reshape(B, L, GLA_HEADS, GLA_DV)
    g = jax.nn.log_sigmoid((jnp.einsum('blr,re->ble', glr, w_gate_up) + b_gate).astype(jnp.float32)) / GLA_TAU
    g = g.reshape(B, L, GLA_HEADS, GLA_DK)
    qf = qf.reshape(B, L, FOX_HEADS, FOX_HD)
    kf = kf.reshape(B, L, FOX_HEADS, FOX_HD)
    vf = vf.reshape(B, L, FOX_HEADS, FOX_HD)
    logf = jax.nn.log_sigmoid((fl + b_fox_f).astype(jnp.float32))
    return qg, kg, vg, g, rg, qf, kf, vf, logf


def gla_chunked(q, k, v, g, s0):
    B, L, H, DK = q.shape
    DV = v.shape[-1]
    pad = (-L) % CHUNK
    if pad:
        q, k, v, g = [jnp.pad(t, ((0, 0), (0, pad), (0, 0), (0, 0))) for t in (q, k, v, g)]
    N = (L + pad) // CHUNK

    def blocks(t):
        return t.reshape(B, N, CHUNK, H, t.shape[-1]).transpose(1, 0, 3, 2, 4)

    qb, kb, vb, gb = blocks(q), blocks(k), blocks(v), blocks(g)
    dt = q.dtype
    b = jnp.cumsum(gb, axis=3)
    b_last = b[:, :, :, -1:, :]
    q_i = (qb * jnp.exp(b)).astype(dt)
    k_i = (kb * jnp.exp(-b)).astype(dt)
    k_e = (kb * jnp.exp(b_last - b)).astype(dt)
    causal = jnp.tril(jnp.ones((CHUNK, CHUNK), dtype=bool))
    att = jnp.where(causal, jnp.einsum('nbhcd,nbhsd->nbhcs', q_i, k_i), 0)
    o_intra = jnp.einsum('nbhcs,nbhsv->nbhcv', att.astype(dt), vb)
    kv = jnp.einsum('nbhsd,nbhsv->nbhdv', k_e, vb).astype(s0.dtype)
    decay = jnp.exp(b_last[:, :, :, 0, :]).astype(s0.dtype)

    def step(S, inp):
        dec, kvn = inp
        return dec[..., None] * S + kvn, S

    s_final, s_starts = lax.scan(step, s0, (decay, kv))
    o_inter = jnp.einsum('nbhcd,nbhdv->nbhcv', q_i, s_starts.astype(dt))
    o = (o_intra + o_inter).transpose(1, 0, 3, 2, 4).reshape(B, N * CHUNK, H, DV)[:, :L]
    return o, s_final


def fox_attend(q, k, v, c_q, c_k, q_pos, k_pos):
    s = jnp.einsum('bqhd,bkhd->bhqk', q, k).astype(jnp.float32) * (FOX_HD ** -0.5)
    s = s + c_q.transpose(0, 2, 1)[:, :, :, None] - c_k.transpose(0, 2, 1)[:, :, None, :]
    s = jnp.where(k_pos[None, :] <= q_pos[:, None], s, -jnp.inf)
    p = jax.nn.softmax(s, axis=-1).astype(v.dtype)
    return jnp.einsum('bhqk,bkhd->bqhd', p, v)


def fox_prompt(q, k, v, logf):
    L = q.shape[1]
    c = jnp.cumsum(logf, axis=1)
    pos = jnp.arange(L)
    outs = []
    for i in range(L // FOX_QBLK):
        lo, hi = i * FOX_QBLK, (i + 1) * FOX_QBLK
        outs.append(fox_attend(q[:, lo:hi], k[:, :hi], v[:, :hi], c[:, lo:hi], c[:, :hi], pos[lo:hi], pos[:hi]))
    return jnp.concatenate(outs, axis=1)


def merge_heads(o_gla, r_g, o_fox, g_onorm, w_out):
    B, L = o_gla.shape[:2]
    o_gla = rmsnorm(o_gla, g_onorm.reshape(GLA_HEADS, GLA_DV)).reshape(B, L, GLA_WIDTH) * jax.nn.silu(r_g)
    o = jnp.concatenate([o_gla, o_fox.reshape(B, L, FOX_WIDTH)], axis=-1)
    return jnp.einsum('ble,ed->bld', o, w_out)


def sq_relu_mlp(h, w_up, w_down):
    u = jax.nn.relu(jnp.einsum('bld,df->blf', h, w_up))
    return jnp.einsum('blf,fd->bld', u * u, w_down)


def setup_inputs(seed: int = 0) -> dict:
    key = jax.random.key(seed)
    ks = jax.random.split(key, 20)

    def nrm(k, shape, scale=1.0):
        return jax.random.normal(k, shape, jnp.float32) * scale

    def gain(k, shape):
        return 1.0 + 0.05 * jax.random.normal(k, shape, jnp.float32)

    return {
        "x_prompt": nrm(ks[0], (BATCH, SEQ, D_MODEL)),
        "x_sample": nrm(ks[1], (DEC_BATCH, DEC_SEQ, D_MODEL)),
        "cache_fox_k": nrm(ks[2], (DEPTH, DEC_BATCH, PAST_LEN, FOX_HEADS, FOX_HD)),
        "cache_fox_v": nrm(ks[3], (DEPTH, DEC_BATCH, PAST_LEN, FOX_HEADS, FOX_HD)),
        "cache_fox_logf": jax.nn.log_sigmoid(nrm(ks[4], (DEPTH, DEC_BATCH, PAST_LEN, FOX_HEADS)) + 2.0),
        "state_gla": nrm(ks[5], (DEPTH, DEC_BATCH, GLA_HEADS, GLA_DK, GLA_DV), 0.5),
        "g_mix_pre": gain(ks[6], (DEPTH, D_MODEL)),
        "w_in": nrm(ks[7], (DEPTH, D_MODEL, PROJ_WIDTH), D_MODEL ** -0.5),
        "w_gla_gate_up": nrm(ks[8], (DEPTH, GLA_RANK, GLA_HEADS * GLA_DK), GLA_RANK ** -0.5),
        "b_gla_gate": nrm(ks[9], (DEPTH, GLA_HEADS * GLA_DK), 0.1),
        "b_fox_f": nrm(ks[10], (DEPTH, FOX_HEADS), 0.1),
        "g_gla_onorm": gain(ks[11], (DEPTH, GLA_WIDTH)),
        "w_out": nrm(ks[12], (DEPTH, MIX_WIDTH, D_MODEL), MIX_WIDTH ** -0.5),
        "g_mix_post": gain(ks[13], (DEPTH, D_MODEL)),
        "g_mlp_pre": gain(ks[14], (DEPTH, D_MODEL)),
        "w_mlp_up": nrm(ks[15], (DEPTH, D_MODEL, D_FF), D_MODEL ** -0.5),
        "w_mlp_down": nrm(ks[16], (DEPTH, D_FF, D_MODEL), D_FF ** -0.5),
        "g_mlp_post": gain(ks[17], (DEPTH, D_MODEL)),
    }


def reference(x_prompt, x_sample, cache_fox_k, cache_fox_v, cache_fox_logf, state_gla,
              g_mix_pre, w_in, w_gla_gate_up, b_gla_gate, b_fox_f, g_gla_onorm, w_out,
              g_mix_post, g_mlp_pre, w_mlp_up, w_mlp_down, g_mlp_post):
    past_len = cache_fox_k.shape[2]
    yp, ys = x_prompt, x_sample
    kp, vp, fp, sp = [], [], [], []
    ksm, vsm, fsm, ssm = [], [], [], []
    for l in range(DEPTH):
        h = rmsnorm(yp, g_mix_pre[l])
        qg, kg, vg, g, rg, qf, kf, vf, logf = project(h, w_in[l], w_gla_gate_up[l], b_gla_gate[l], b_fox_f[l])
        s0 = jnp.zeros((yp.shape[0], GLA_HEADS, GLA_DK, GLA_DV), yp.dtype)
        o_gla, s_new = gla_chunked(qg, kg, vg, g, s0)
        o_fox = fox_prompt(qf, kf, vf, logf)
        yp = yp + rmsnorm(merge_heads(o_gla, rg, o_fox, g_gla_onorm[l], w_out[l]), g_mix_post[l])
        yp = yp + rmsnorm(sq_relu_mlp(rmsnorm(yp, g_mlp_pre[l]), w_mlp_up[l], w_mlp_down[l]), g_mlp_post[l])
        kp.append(kf); vp.append(vf); fp.append(logf); sp.append(s_new)

        h = rmsnorm(ys, g_mix_pre[l])
        qg, kg, vg, g, rg, qf, kf, vf, logf = project(h, w_in[l], w_gla_gate_up[l], b_gla_gate[l], b_fox_f[l])
        o_gla, s_new = gla_chunked(qg, kg, vg, g, state_gla[l])
        T = qf.shape[1]
        k_all = jnp.concatenate([cache_fox_k[l], kf], axis=1)
        v_all = jnp.concatenate([cache_fox_v[l], vf], axis=1)
        c_all = jnp.cumsum(jnp.concatenate([cache_fox_logf[l].astype(jnp.float32), logf], axis=1), axis=1)
        o_fox = fox_attend(qf, k_all, v_all, c_all[:, past_len:], c_all,
                           past_len + jnp.arange(T), jnp.arange(past_len + T))
        ys = ys + rmsnorm(merge_heads(o_gla, rg, o_fox, g_gla_onorm[l], w_out[l]), g_mix_post[l])
        ys = ys + rmsnorm(sq_relu_mlp(rmsnorm(ys, g_mlp_pre[l]), w_mlp_up[l], w_mlp_down[l]), g_mlp_post[l])
        ksm.append(kf); vsm.append(vf); fsm.append(logf); ssm.append(s_new)

    new_fox_k_prompt = jnp.stack(kp)
    new_fox_v_prompt = jnp.stack(vp)
    new_fox_logf_prompt = jnp.stack(fp)
    new_state_gla_prompt = jnp.stack(sp)
    new_fox_k_sample = jnp.stack(ksm)
    new_fox_v_sample = jnp.stack(vsm)
    new_fox_logf_sample = jnp.stack(fsm)
    new_state_gla_sample = jnp.stack(ssm)
    return (yp, ys, new_fox_k_prompt, new_fox_v_prompt, new_fox_logf_prompt, new_state_gla_prompt,
            new_fox_k_sample, new_fox_v_sample, new_fox_logf_sample, new_state_gla_sample)
```

```python
import numpy as np
import concourse.bass as bass
import concourse.mybir as mybir
from concourse.bass_utils import run_bass_kernel_spmd

F32 = mybir.dt.float32
BF16 = mybir.dt.bfloat16
U8 = mybir.dt.uint8
AF = mybir.ActivationFunctionType
ALU = mybir.AluOpType

D = 2048
T = 1024
S = 32
TT = T + S
NL = 2
PW = 6168
DFF = 8192
TILES = [(0, 512), (512, 512), (1024, 32)]
MTILES = [(0, 352), (352, 352), (704, 352)]
BLKS = [(i * 128, 128) for i in range(8)] + [(1024, 32)]
NT256 = BLKS
EPS = 1e-6
NEG = -1.0e30
SAME_ENGINE_SYNC = True
STOP = 0


class _Stop(Exception):
    pass


DUMP = 0
_DUMP_FN = [None]


def chk(n):
    if DUMP == n:
        _DUMP_FN[0](n)
        raise _Stop()
    if STOP == n:
        raise _Stop()

_DSZ = {F32: 4, BF16: 2, U8: 1}


def _is_ap(v):
    return isinstance(v, bass.AP)


def _space(ap):
    n = type(ap.tensor).__name__
    if n.startswith("SB"):
        return "sb"
    if n.startswith("PS"):
        return "ps"
    return "dram"


CELL = 64


def _cells(ap):
    es = _DSZ[ap.dtype]
    dims = list(ap.ap)
    pstep = dims[0][0]
    off = int(ap.offset) % pstep if pstep else int(ap.offset)
    free = [(s, c) for (s, c) in dims[1:] if c > 1 and s != 0]
    free.sort(key=lambda x: -abs(x[0]))
    if free and free[-1][0] == 1:
        run = free[-1][1]
        outer = free[:-1]
    else:
        run = 1
        outer = free
    nout = 1
    for s, c in outer:
        nout *= c
    res = set()
    if nout > 256:
        ext = sum((c - 1) * s for s, c in free) + 1
        res.update(range((off * es) // CELL, ((off + ext) * es - 1) // CELL + 1))
        return res
    starts = [off]
    for s, c in outer:
        starts = [b + i * s for b in starts for i in range(c)]
    for b in starts:
        res.update(range((b * es) // CELL, ((b + run) * es - 1) // CELL + 1))
    return res


class Op:
    __slots__ = ("eng", "fn", "deps", "dma", "sig", "sem", "val", "cc")

    def __init__(self, eng, fn, dma, cc=False):
        self.eng = eng
        self.fn = fn
        self.deps = set()
        self.dma = dma
        self.sig = False
        self.sem = None
        self.val = 0
        self.cc = cc


class Prog:
    ENGS = ("pe", "act", "dve", "pool", "sp")

    def __init__(self, nc):
        self.nc = nc
        self.ops = []
        self.state = {"sb": {}, "ps": {}}
        self.dstate = {}

    def _touch(self, oid, reads, writes, dreads, dwrites):
        op = self.ops[oid]
        deps = op.deps
        for ap in reads:
            sp_ = _space(ap)
            st = self.state[sp_]
            if sp_ == "ps":
                for c in {c * CELL // 2048 for c in _cells(ap)}:
                    e = st.get(c)
                    if e is None:
                        st[c] = [None, [oid]]
                    else:
                        if e[0] is not None:
                            deps.add(e[0])
                        for r in e[1]:
                            if self.ops[r].eng != op.eng:
                                deps.add(r)
                        e[1].append(oid)
                continue
            for c in _cells(ap):
                e = st.get(c)
                if e is None:
                    st[c] = [None, [oid]]
                else:
                    if e[0] is not None:
                        deps.add(e[0])
                    e[1].append(oid)
        for ap in writes:
            sp_ = _space(ap)
            st = self.state[sp_]
            cl = _cells(ap)
            if sp_ == "ps":
                cl = {c * CELL // 2048 for c in cl}
            for c in cl:
                e = st.get(c)
                if e is None:
                    st[c] = [oid, []]
                else:
                    if e[0] is not None:
                        deps.add(e[0])
                    deps.update(e[1])
                    e[0] = oid
                    e[1] = []
        for k in dreads:
            e = self.dstate.setdefault(k, [None, []])
            if e[0] is not None:
                deps.add(e[0])
            e[1].append(oid)
        for k in dwrites:
            e = self.dstate.setdefault(k, [None, []])
            if e[0] is not None:
                deps.add(e[0])
            deps.update(e[1])
            e[0] = oid
            e[1] = []
        deps.discard(oid)

    def add(self, eng, fn, reads=(), writes=(), dreads=(), dwrites=(), dma=False, cc=False):
        oid = len(self.ops)
        self.ops.append(Op(eng, fn, dma, cc))
        self._touch(oid, reads, writes, dreads, dwrites)
        return oid

    def I(self, eng, meth, **kw):
        writes = [v for k, v in kw.items() if k in ("out", "accum_out", "ap") and _is_ap(v)]
        reads = [v for k, v in kw.items() if k not in ("out", "accum_out", "ap") and _is_ap(v)]
        reads = [r for r in reads if _space(r) != "dram"]
        return self.add(eng, lambda e: getattr(e, meth)(**kw), reads, writes)

    def mm(self, out, lhsT, rhs, start, stop):
        return self.add("pe", lambda e: e.matmul(out, lhsT, rhs, start=start, stop=stop),
                        [lhsT, rhs], [out])

    def tr(self, out, in_, ident):
        return self.add("pe", lambda e: e.transpose(out, in_, ident), [in_, ident], [out])

    def dma(self, q, out, in_, dreads=(), dwrites=(), noncontig=False):
        reads = [in_] if _space(in_) != "dram" else []
        writes = [out] if _space(out) != "dram" else []
        nc = self.nc
        if noncontig:
            def fn(e):
                with nc.allow_non_contiguous_dma(reason="small strided parameter load"):
                    return e.dma_start(out=out, in_=in_)
        else:
            def fn(e):
                return e.dma_start(out=out, in_=in_)
        return self.add(q, fn, reads, writes, dreads, dwrites, dma=True)

    def cc(self, ins, outs, groups, dreads, dwrites):
        import os
        if os.environ.get('KNOCC'):
            return None
        def fn(e):
            return e.collective_compute("AllGather", ALU.bypass, replica_groups=groups,
                                        ins=ins, outs=outs)
        return self.add("pool", fn, [], [], dreads, dwrites, dma=True, cc=True)

    def emit(self, out_final_keys=()):
        nc = self.nc
        ops = self.ops
        NS = 20
        for o in ops:
            for d in o.deps:
                p = ops[d]
                if p.dma:
                    continue
                if p.eng == o.eng and not o.dma:
                    if p.eng == "pe" or not SAME_ENGINE_SYNC:
                        continue
                p.sig = True
        import contextlib
        with contextlib.ExitStack() as es:
            esem = {e: es.enter_context(nc.semaphore("s_" + e)) for e in ("pe", "act", "dve", "pool")}
            dsem = {q: [es.enter_context(nc.semaphore("d_%s_%d" % (q, i))) for i in range(NS)]
                    for q in ("sp", "pool", "act")}
            ncc = sum(1 for o in ops if o.cc)
            csem = [es.enter_context(nc.semaphore("cc_%d" % i)) for i in range(ncc)]
            cnt = {e: 0 for e in esem}
            dcnt = {q: 0 for q in dsem}
            ci = 0
            prewait = {}
            for i, o in enumerate(ops):
                if o.cc:
                    o.sem = csem[ci]
                    o.val = 1
                    ci += 1
                elif o.dma:
                    n = dcnt[o.eng]
                    dcnt[o.eng] += 1
                    o.sem = dsem[o.eng][n % NS]
                    o.val = 16 * (n // NS + 1)
                    if n >= NS:
                        prewait[i] = (o.sem, o.val - 16)
                elif o.sig:
                    cnt[o.eng] += 1
                    o.sem = esem[o.eng]
                    o.val = cnt[o.eng]
            per = {e: [] for e in self.ENGS}
            for i, o in enumerate(ops):
                per[o.eng].append(i)
            finals = {}
            for o in ops:
                if o.dma:
                    k = id(o.sem)
                    if k not in finals or finals[k][1] < o.val:
                        finals[k] = (o.sem, o.val)
            blk = es.enter_context(nc.Block())

            def run(engname, eng):
                waited = {}
                for i in per[engname]:
                    o = ops[i]
                    need = {}
                    if i in prewait:
                        s, v = prewait[i]
                        need[id(s)] = (s, v)
                    for d in o.deps:
                        p = ops[d]
                        if not p.dma:
                            if p.eng == o.eng and not o.dma and (p.eng == "pe" or not SAME_ENGINE_SYNC):
                                continue
                        k = id(p.sem)
                        if k not in need or need[k][1] < p.val:
                            need[k] = (p.sem, p.val)
                    for k, (s, v) in need.items():
                        if waited.get(k, 0) >= v:
                            continue
                        eng.wait_ge(s, v)
                        waited[k] = v
                    ins = o.fn(eng)
                    if o.cc:
                        ins.then_inc(o.sem, 1)
                    elif o.dma:
                        ins.then_inc(o.sem, 16)
                    elif o.sig:
                        ins.then_inc(o.sem, 1)
                if engname == "sp":
                    for k, (s, v) in finals.items():
                        if waited.get(k, 0) < v:
                            eng.wait_ge(s, v)

            @blk.sync
            def _(e):
                run("sp", e)

            @blk.gpsimd
            def _(e):
                run("pool", e)

            @blk.tensor
            def _(e):
                run("pe", e)

            @blk.scalar
            def _(e):
                run("act", e)

            @blk.vector
            def _(e):
                run("dve", e)


class Arena:
    def __init__(self, sb, nbytes):
        self.sb = sb
        self.n = nbytes
        self.top = 0
        self.peak = 0

    def alloc(self, dtype, shape, parts=128):
        es = _DSZ[dtype]
        n = es
        for s in shape:
            n *= s
        off = (self.top + CELL - 1) // CELL * CELL
        assert off + n <= self.n, ("SBUF arena overflow", off, n, self.n)
        self.top = off + n
        self.peak = max(self.peak, self.top)
        ap = self.sb[:, off:off + n].bitcast(dtype)
        if len(shape) == 2:
            ap = ap.rearrange("p (a b) -> p a b", a=shape[0])
        elif len(shape) == 3:
            ap = ap.rearrange("p (a b c) -> p a b c", a=shape[0], b=shape[1])
        return ap

    def mark(self):
        return self.top

    def release(self, m):
        self.top = m


C_ID = 0
C_GM = 128
C_DM = 256
C_ONE = 384
C_OD = 512
C_OV = 640
C_SC = 768
C_ON = 768 + TT
NCST = 768 + TT


def make_consts():
    c = np.zeros((128, NCST), np.float32)
    i = np.arange(128)
    c[:, C_ID:C_ID + 128] = np.eye(128, dtype=np.float32)
    s = i[:, None]
    q = i[None, :]
    c[:, C_GM:C_GM + 128] = (s <= q).astype(np.float32)
    c[:, C_DM:C_DM + 128] = np.where(s <= q, 0.0, NEG).astype(np.float32)
    c[:, C_ONE:C_ONE + 128] = 1.0
    c[:, C_OD:C_OD + 128] = 1.0 / 2048.0
    c[:, C_OV:C_OV + 128] = 1.0 / 256.0
    m = np.ones(TT, np.float32)
    m[::128] = 0.0
    c[:, C_SC:C_SC + TT] = m[None, :]
    return c


def make_sel():
    s = np.zeros((8, 8, 128), np.float32)
    for h in range(8):
        s[h, h, :] = 1.0
    return s.reshape(8, 1024)


def build(nlayers=NL):
    nc = bass.Bass("TRN2", target_bir_lowering=False)
    dt = nc.dram_tensor

    def din(name, shape):
        return dt(name, list(shape), F32, kind="ExternalInput").ap()

    def dout(name, shape):
        return dt(name, list(shape), F32, kind="ExternalOutput").ap()

    xp = din("xp", [T, D]); xs = din("xs", [S, D])
    ck = din("ck", [NL, 2048, 1024]); cv = din("cv", [NL, 2048, 1024])
    clf = din("clf", [NL, 2048, 8]); sg = din("sg", [NL, 4, 128, 256])
    flag = din("flag", [128, 1]); cst = din("cst", [128, NCST])
    gmp = din("gmp", [NL, D]); w_in = din("w_in", [NL, D, PW]); w2 = din("w2", [NL, 16, 512])
    bg = din("bg", [NL, 512]); bfx = din("bfx", [NL, 8]); gon = din("gon", [NL, 1024])
    w_out = din("w_out", [NL, D, D]); gpost = din("gpost", [NL, D]); gmlp = din("gmlp", [NL, D])
    w_up = din("w_up", [NL, D, DFF]); w_dn = din("w_dn", [NL, DFF, D]); gmpost = din("gmpost", [NL, D])
    yp = dout("yp", [T, D]); ys = dout("ys", [S, D])
    kp = dout("kp", [NL, T, 1024]); vp = dout("vp", [NL, T, 1024]); fp = dout("fp", [NL, T, 8])
    spo = dout("spo", [NL, 4, 128, 256])
    kso = dout("kso", [NL, S, 1024]); vso = dout("vso", [NL, S, 1024]); fso = dout("fso", [NL, S, 8])
    sso = dout("sso", [NL, 4, 128, 256])
    xres = dt("xres", [128, 16, TT], F32, kind="Internal").ap()
    cin_k = [dt("cin_k%d" % l, [128, 8192], BF16, kind="Internal").ap() for l in range(NL)]
    cout_k = [dt("cout_k%d" % l, [256, 8192], BF16, kind="Internal").ap() for l in range(NL)]
    cin_v = [dt("cin_v%d" % l, [128, 8192], BF16, kind="Internal").ap() for l in range(NL)]
    cout_v = [dt("cout_v%d" % l, [256, 8192], BF16, kind="Internal").ap() for l in range(NL)]
    NF = 1024 + 64
    cin_f = [dt("cin_f%d" % l, [128, NF], F32, kind="Internal").ap() for l in range(NL)]
    cout_f = [dt("cout_f%d" % l, [256, NF], F32, kind="Internal").ap() for l in range(NL)]
    GROUPS = [[0, 1], [2, 3], [4, 5], [6, 7]]
    qscr = dt("qscr", [128, 8, TT], BF16, kind="Internal").ap()
    gscr = dt("gscr", [128, 8, TT], BF16, kind="Internal").ap()

    import contextlib
    es = contextlib.ExitStack()
    ARENA_BYTES = 206 * 1024
    sb = es.enter_context(nc.sbuf_tensor("arena", [128, ARENA_BYTES], U8))
    pst = es.enter_context(nc.psum_tensor("pst", [128, 8, 512], F32))
    A = Arena(sb, ARENA_BYTES)
    P = Prog(nc)
    bank_ctr = [0]
    ll_top = [0]

    def bank():
        b = bank_ctr[0] % 8
        bank_ctr[0] += 1
        return pst[:, b, :]

    bank6_ctr = [0]

    def bank6():
        b = bank6_ctr[0] % 6
        bank6_ctr[0] += 1
        return pst[:, b, :]

    def bank_bf(b_ap):
        return b_ap.bitcast(BF16)

    cs = A.alloc(F32, [NCST])
    P.dma("sp", cs, cst)
    ident_f = cs[:, C_ID:C_ID + 128]
    gmask = cs[:, C_GM:C_GM + 128]
    dmask = cs[:, C_DM:C_DM + 128]
    ones_f = cs[:, C_ONE:C_ONE + 128]
    onesD = cs[:, C_OD:C_OD + 128]
    onesV = cs[:, C_OV:C_OV + 128]
    scanm = cs[:, C_SC:C_SC + TT]
    ident_b = A.alloc(BF16, [128])
    P.I("dve", "tensor_copy", out=ident_b, in_=ident_f)
    ones_b = A.alloc(BF16, [128])
    P.I("dve", "tensor_copy", out=ones_b, in_=ones_f)
    flg = A.alloc(F32, [1])
    P.dma("sp", flg, flag)
    epst = A.alloc(F32, [1])
    P.I("dve", "memset", ap=epst, constant=EPS)
    negbig = A.alloc(F32, [1])
    P.I("dve", "tensor_scalar", out=negbig, in0=flg, scalar1=-1.0, scalar2=1.0e30, op0=ALU.add, op1=ALU.mult)

    prow = A.alloc(F32, [128])

    def pvec(src, n):
        t = A.alloc(F32, [NL, n])
        r = NL * n
        P.dma("sp", prow[0:r, :], src.rearrange("l (c p) -> (l c) p", p=128))
        pbk = bank()
        P.tr(pbk[:, 0:r], prow[0:r, :], ident_f[0:r, 0:r])
        P.I("act", "activation", out=t.rearrange("p l c -> p (l c)"), in_=pbk[:, 0:r], func=AF.Copy)
        return t
    g_pre = pvec(gmp, 16); g_post = pvec(gpost, 16); g_mlp = pvec(gmlp, 16); g_mpost = pvec(gmpost, 16)
    g_on = pvec(gon, 8)
    nbg = pvec(bg, 4)
    P.I("dve", "tensor_scalar", out=nbg, in0=nbg, scalar1=-1.0, scalar2=None, op0=ALU.mult)
    nbf = A.alloc(F32, [NL])
    P.dma("sp", prow[0:NL, 0:8], bfx)
    pbk = bank()
    P.tr(pbk[0:8, 0:NL], prow[0:NL, 0:8], ident_f[0:NL, 0:NL])
    P.I("dve", "tensor_scalar", out=nbf[0:8, :], in0=pbk[0:8, 0:NL], scalar1=-1.0, scalar2=None, op0=ALU.mult)
    w2t = A.alloc(F32, [NL, 512])
    P.dma("sp", w2t[0:16, :, :], w2.rearrange("l r e -> r l e"))

    act = A.alloc(BF16, [16, TT])
    WSLOT = 2
    wbuf = [A.alloc(BF16, [8192]) for _ in range(WSLOT)]
    wctr = [0]

    def wslot():
        w = wbuf[wctr[0] % WSLOT]
        wctr[0] += 1
        return w

    def load_w_cols(src2d, c0, n):
        w = wslot()[:, 0:16 * n].rearrange("p (c n) -> p c n", c=16)
        sv = src2d[:, c0:c0 + n].rearrange("(c p) n -> p c n", p=128)
        for q in range(4):
            P.dma("pool", w[:, q * 4:(q + 1) * 4, :], sv[:, q * 4:(q + 1) * 4, :])
        return w

    def load_w_rows(src2d, r0, nchunks, ncols):
        w = wslot()[:, 0:nchunks * ncols].rearrange("p (c n) -> p c n", c=nchunks)
        sv = src2d[r0:r0 + nchunks * 128, :].rearrange("(c p) n -> p c n", p=128)
        for q in range(nchunks):
            P.dma("pool", w[:, q:q + 1, :], sv[:, q:q + 1, :])
        return w

    def proj_fm(w, ncol, evac, m_chunk=128, pre=None, tiles=MTILES):
        ne = (ncol + m_chunk - 1) // m_chunk
        for e in range(ne):
            m = min(m_chunk, ncol - e * m_chunk)
            if pre is not None:
                pre(e)
            banks = [bank() for _ in tiles]
            for d in range(16):
                for ti, (t0, tn) in enumerate(tiles):
                    P.mm(banks[ti][0:m, 0:tn], w[:, d, e * m_chunk:e * m_chunk + m], act[:, d, t0:t0 + tn],
                         d == 0, d == 15)
            for ti, (t0, tn) in enumerate(tiles):
                evac(e, ti, t0, tn, banks[ti][0:m, 0:tn])

    def proj_tm(w, ncol, evac):
        for bi, (b0, bn) in enumerate(BLKS):
            pb = bank()
            for d in range(16):
                P.mm(pb[0:bn, 0:ncol], act[:, d, b0:b0 + bn], w[:, d, 0:ncol], d == 0, d == 15)
            evac(bi, b0, bn, pb[0:bn, 0:ncol])

    def rms_rstd(src_sq_chunks, w, ones_m, nchunk, out_rstd):
        pb = bank()
        for c in range(nchunk):
            P.mm(pb[:, 0:w], ones_m, src_sq_chunks(c), c == 0, c == nchunk - 1)
        P.I("act", "activation", out=out_rstd, in_=pb[:, 0:w], func=AF.Sqrt, bias=epst[:, 0:1], scale=1.0)
        P.I("dve", "reciprocal", out=out_rstd, in_=out_rstd)

    def norm_from_xres(gt, l):
        m = A.mark()
        NX = 3
        xin = [A.alloc(F32, [16, 128]) for _ in range(NX)]
        sqs = [A.alloc(F32, [16, 128]) for _ in range(2)]
        rstds = [A.alloc(F32, [128]) for _ in range(2)]
        for i, (t0, tn) in enumerate(NT256):
            x = xin[i % NX]
            sq = sqs[i % 2]
            rstd = rstds[i % 2]
            P.dma("sp", x[:, :, 0:tn], xres[:, :, t0:t0 + tn], dreads=[("xres", i)])
            P.I("act", "activation", out=sq[:, :, 0:tn], in_=x[:, :, 0:tn], func=AF.Square)
            rms_rstd(lambda c: sq[:, c, 0:tn], tn, onesD, 16, rstd[:, 0:tn])
            for c in range(16):
                eng = "dve"
                P.I(eng, "scalar_tensor_tensor", out=act[:, c, t0:t0 + tn], in0=x[:, c, 0:tn],
                    scalar=gt[:, l, c:c + 1], in1=rstd[:, 0:tn], op0=ALU.mult, op1=ALU.mult)
        A.release(m)

    def postnorm_residual(yT, gt, l, final, nxt=None):
        m = A.mark()
        NX = 2 if final else 3
        NQ = 1 if final else 2
        xin = [A.alloc(F32, [16, 128]) for _ in range(NX)]
        sqs = [A.alloc(F32, [16, 128]) for _ in range(NQ)]
        rstds = [A.alloc(F32, [128]) for _ in range(2)]
        rstds2 = [A.alloc(F32, [128]) for _ in range(2)]
        stage = [A.alloc(F32, [2048]) for _ in range(2)] if final else None
        sctr = 0
        for i, (t0, tn) in enumerate(NT256):
            x = xin[i % NX]
            sq = sqs[i % NQ]
            rstd = rstds[i % 2]
            P.dma("sp", x[:, :, 0:tn], xres[:, :, t0:t0 + tn], dreads=[("xres", i)])
            P.I("act", "activation", out=sq[:, :, 0:tn], in_=yT[:, :, t0:t0 + tn], func=AF.Square)
            rms_rstd(lambda c: sq[:, c, 0:tn], tn, onesD, 16, rstd[:, 0:tn])
            for c in range(16):
                eng = "dve"
                P.I(eng, "scalar_tensor_tensor", out=sq[:, c, 0:tn], in0=yT[:, c, t0:t0 + tn],
                    scalar=gt[:, l, c:c + 1], in1=rstd[:, 0:tn], op0=ALU.mult, op1=ALU.mult)
            P.I("dve", "tensor_tensor", out=x[:, 0:10, 0:tn], in0=x[:, 0:10, 0:tn], in1=sq[:, 0:10, 0:tn], op=ALU.add)
            P.I("pool", "tensor_tensor", out=x[:, 10:16, 0:tn], in0=x[:, 10:16, 0:tn], in1=sq[:, 10:16, 0:tn], op=ALU.add)
            if nxt is not None:
                ngt, nl = nxt
                rstd2 = rstds2[i % 2]
                P.I("act", "activation", out=sq[:, :, 0:tn], in_=x[:, :, 0:tn], func=AF.Square)
                rms_rstd(lambda c: sq[:, c, 0:tn], tn, onesD, 16, rstd2[:, 0:tn])
                for c in range(16):
                    P.I("dve", "scalar_tensor_tensor", out=act[:, c, t0:t0 + tn], in0=x[:, c, 0:tn],
                        scalar=ngt[:, nl, c:c + 1], in1=rstd2[:, 0:tn], op0=ALU.mult, op1=ALU.mult)
            if not final:
                P.dma("sp", xres[:, :, t0:t0 + tn], x[:, :, 0:tn], dwrites=[("xres", i)])
            else:
                for sb0 in range(0, tn, 128):
                    bn = min(128, tn - sb0)
                    st = stage[sctr % 2]
                    sctr += 1
                    for c4 in range(4):
                        pb = bank()
                        for k in range(4):
                            c = c4 * 4 + k
                            P.tr(pb[0:bn, k * 128:(k + 1) * 128], x[:, c, sb0:sb0 + bn], ident_f)
                        P.I("act", "activation", out=st[0:bn, c4 * 512:(c4 + 1) * 512], in_=pb[0:bn, :], func=AF.Copy)
                    tok = t0 + sb0
                    if tok < T:
                        P.dma("sp", yp[tok:tok + bn, :], st[0:bn, :])
                    else:
                        P.dma("sp", ys[0:bn, :], st[0:bn, :])
        A.release(m)

    def dump_x(n):
        m = A.mark()
        xin = [A.alloc(F32, [16, 128]) for _ in range(2)]
        stage = [A.alloc(F32, [2048]) for _ in range(2)]
        for i, (t0, tn) in enumerate(BLKS):
            x = xin[i % 2]
            st = stage[i % 2]
            if n == 5:
                P.I("dve", "tensor_copy", out=x[:, :, 0:tn], in_=act[:, :, t0:t0 + tn])
            else:
                P.dma("sp", x[:, :, 0:tn], xres[:, :, t0:t0 + tn], dreads=[("xres", i)])
            for c4 in range(4):
                pb = bank()
                for k in range(4):
                    c = c4 * 4 + k
                    P.tr(pb[0:tn, k * 128:(k + 1) * 128], x[:, c, 0:tn], ident_f)
                P.I("act", "activation", out=st[0:tn, c4 * 512:(c4 + 1) * 512], in_=pb[0:tn, :], func=AF.Copy)
            if t0 < T:
                P.dma("sp", yp[t0:t0 + tn, :], st[0:tn, :])
            else:
                P.dma("sp", ys[0:tn, :], st[0:tn, :])
        A.release(m)
    _DUMP_FN[0] = dump_x

    m0 = A.mark()
    xtok = [A.alloc(F32, [2048]) for _ in range(2)]
    xst = [A.alloc(F32, [16, 128]) for _ in range(3)]
    sq0 = [A.alloc(F32, [16, 128]) for _ in range(2)]
    rs0 = [A.alloc(F32, [128]) for _ in range(2)]
    for bi, (b0, bn) in enumerate(BLKS):
        xt = xtok[bi % 2]
        st = xst[bi % 3]
        if b0 < T:
            P.dma("sp", xt[0:bn, :], xp[b0:b0 + bn, :])
        else:
            P.dma("sp", xt[0:bn, :], xs[0:bn, :])
        for c4 in range(4):
            pb = bank()
            for k in range(4):
                c = c4 * 4 + k
                P.tr(pb[:, k * 128:k * 128 + bn], xt[0:bn, c * 128:(c + 1) * 128], ident_f[0:bn, 0:bn])
            P.I("act", "activation", out=st[:, c4 * 4:(c4 + 1) * 4, 0:bn],
                in_=pb.rearrange("p (k n) -> p k n", k=4)[:, :, 0:bn], func=AF.Copy)
        P.dma("sp", xres[:, :, b0:b0 + bn], st[:, :, 0:bn], dwrites=[("xres", bi)])
        sq = sq0[bi % 2]
        rstd = rs0[bi % 2]
        P.I("act", "activation", out=sq[:, :, 0:bn], in_=st[:, :, 0:bn], func=AF.Square)
        rms_rstd(lambda c: sq[:, c, 0:bn], bn, onesD, 16, rstd[:, 0:bn])
        for c in range(16):
            P.I("dve", "scalar_tensor_tensor", out=act[:, c, b0:b0 + bn], in0=st[:, c, 0:bn],
                scalar=g_pre[:, 0, c:c + 1], in1=rstd[:, 0:bn], op0=ALU.mult, op1=ALU.mult)
    A.release(m0)

    def layers():
        for l in range(nlayers):
            final_layer = (l == nlayers - 1)
            chk(99)
            chk(98)
            mL = A.mark()
            win = w_in[l]
            KTs = A.alloc(BF16, [8, S])
            Vs = A.alloc(BF16, [1024])
            lfT = A.alloc(F32, [TT])
            cT = A.alloc(F32, [TT])
            lf_tok = A.alloc(F32, [9, 8])
            negc = A.alloc(F32, [9, 8])
            Rown = A.alloc(F32, [8, 8])
            decay = A.alloc(F32, [4, 9])
            S32 = A.alloc(F32, [4, 256])
            S32s = A.alloc(F32, [4, 256])
            Sb = [A.alloc(BF16, [4, 256]) for _ in range(2)]
            ostage = [A.alloc(F32, [512]) for _ in range(2)]
            bstage = [A.alloc(BF16, [TT]) for _ in range(2)]
            bctr = [0]
            ll_top[0] = A.top

            for half in range(2):
                w = load_w_cols(win, 4112 + half * 512, 512)

                def ev_k(e, ti, t0, tn, ps, half=half):
                    h = half * 4 + e
                    if ti == 0:
                        bctr[0] += 1
                    st = bstage[bctr[0] % 2]
                    if t0 < T:
                        P.I("act", "activation", out=st[:, t0:t0 + tn], in_=ps, func=AF.Copy)
                    else:
                        P.I("act", "activation", out=KTs[:, h, :], in_=ps, func=AF.Copy)
                        P.dma("sp", cin_k[l][:, h * 1024:(h + 1) * 1024], st[:, 0:T], dwrites=[("cin_kv", l, "k", h)])
                proj_fm(w, 512, ev_k, tiles=TILES)
                chk(97)

                def ev_ktm(bi, b0, bn, ps, half=half):
                    st = ostage[bi % 2]
                    P.I("act", "activation", out=st[0:bn, :], in_=ps, func=AF.Copy)
                    if b0 < T:
                        P.dma("sp", kp[l, b0:b0 + bn, half * 512:(half + 1) * 512], st[0:bn, :])
                    else:
                        P.dma("sp", kso[l, 0:bn, half * 512:(half + 1) * 512], st[0:bn, :])
                proj_tm(w, 512, ev_ktm)
                chk(96 - half)
            cinV = cin_v[l].rearrange("p (b e) -> p b e", b=8)
            for half in range(2):
                w = load_w_cols(win, 5136 + half * 512, 512)

                def ev_v(bi, b0, bn, ps, half=half):
                    st = ostage[bi % 2]
                    P.I("act", "activation", out=st[0:bn, :], in_=ps, func=AF.Copy)
                    if b0 < T:
                        bctr[0] += 1
                        sb_ = bstage[bctr[0] % 2]
                        P.I("dve", "tensor_copy", out=sb_[:, 0:512], in_=ps)
                        P.dma("sp", cinV[:, bi, half * 512:(half + 1) * 512], sb_[:, 0:512],
                              dwrites=[("cin_kv", l, "v", bi, half)])
                        P.dma("sp", vp[l, b0:b0 + bn, half * 512:(half + 1) * 512], st[0:bn, :])
                    else:
                        P.I("dve", "tensor_copy", out=Vs[0:bn, half * 512:(half + 1) * 512], in_=ps)
                        P.dma("sp", vso[l, 0:bn, half * 512:(half + 1) * 512], st[0:bn, :])
                proj_tm(w, 512, ev_v)
                chk(94 - half)
            P.cc([cin_k[l]], [cout_k[l]], GROUPS, dreads=[("cin_kv", l, "k", h) for h in range(8)],
                 dwrites=[("cout_k", l)])
            P.cc([cin_v[l]], [cout_v[l]], GROUPS,
                 dreads=[("cin_kv", l, "v", b, hf) for b in range(8) for hf in range(2)], dwrites=[("cout_v", l)])
            chk(1)

            wfl = wslot()[:, 0:128].rearrange("p (c n) -> p c n", c=16)
            P.dma("pool", wfl, win[:, 6160:6168].rearrange("(c p) n -> p c n", p=128))
            mt = A.mark()
            etmp = A.alloc(F32, [TT])

            def ev_fl(e, ti, t0, tn, ps):
                P.I("act", "activation", out=etmp[0:8, t0:t0 + tn], in_=ps, func=AF.Exp,
                    bias=nbf[0:8, l:l + 1], scale=-1.0)
            proj_fm(wfl, 8, ev_fl, m_chunk=8)
            P.I("act", "activation", out=lfT[0:8, :], in_=etmp[0:8, :], func=AF.Ln, bias=1.0, scale=1.0)
            P.I("dve", "tensor_scalar", out=lfT[0:8, :], in0=lfT[0:8, :], scalar1=-1.0, scalar2=None, op0=ALU.mult)
            onesr = A.alloc(F32, [T])
            P.I("dve", "memset", ap=onesr[0:8, :], constant=1.0)
            P.I("dve", "tensor_tensor_scan", out=cT[0:8, 0:T], data0=onesr[0:8, 0:T],
                data1=lfT[0:8, 0:T], initial=0.0, op0=ALU.mult, op1=ALU.add)
            P.I("dve", "tensor_tensor_scan", out=cT[0:8, T:TT], data0=onesr[0:8, 0:S],
                data1=lfT[0:8, T:TT], initial=0.0, op0=ALU.mult, op1=ALU.add)
            A.release(mt)
            for bi, (b0, bn) in enumerate(BLKS):
                pb = bank()
                P.tr(pb[0:bn, 0:8], lfT[0:8, b0:b0 + bn], ident_f[0:8, 0:8])
                P.tr(pb[0:bn, 8:16], cT[0:8, b0:b0 + bn], ident_f[0:8, 0:8])
                P.I("act", "activation", out=lf_tok[0:bn, bi, :], in_=pb[0:bn, 0:8], func=AF.Copy)
                P.I("dve", "tensor_scalar", out=negc[0:bn, bi, :], in0=pb[0:bn, 8:16], scalar1=-1.0, scalar2=None,
                    op0=ALU.mult)
            P.dma("sp", fp[l].rearrange("(b p) h -> p b h", p=128), lf_tok[:, 0:8, :])
            P.dma("sp", fso[l], lf_tok[0:S, 8, :])
            mt = A.mark()
            cl8 = A.alloc(F32, [128])
            P.I("dve", "tensor_copy", out=cl8[0:8, :], in_=cT[0:8, T - 1:T].to_broadcast([8, 128]))
            pb = bank()
            P.mm(pb[:, 0:8], cl8[0:8, :], ident_f[0:8, 0:8], True, True)
            ctb = A.alloc(F32, [8])
            P.I("act", "activation", out=ctb, in_=pb[:, 0:8], func=AF.Copy)
            for b in range(8):
                P.I("dve", "tensor_tensor", out=Rown[:, b, :], in0=negc[:, b, :], in1=ctb, op=ALU.add)
            A.release(mt)
            P.dma("sp", cin_f[l][:, 1024:1088].rearrange("p (b h) -> p b h", b=8), Rown,
                  dwrites=[("cin_f", l, 1)])

            chk(2)
            qi = A.alloc(BF16, [4, TT])
            ki = A.alloc(BF16, [4, TT])
            ke_tok = A.alloc(BF16, [9, 512])
            mG = A.mark()
            wgl = wslot()[:, 0:256].rearrange("p (c n) -> p c n", c=16)
            P.dma("pool", wgl, win[:, 2048:2064].rearrange("(c p) n -> p c n", p=128))
            glrT = A.alloc(F32, [TT])

            def ev_glr(e, ti, t0, tn, ps):
                P.I("act", "activation", out=glrT[0:16, t0:t0 + tn], in_=ps, func=AF.Copy)
            proj_fm(wgl, 16, ev_glr, m_chunk=16)
            Bc = A.alloc(F32, [4, TT])
            keT = A.alloc(BF16, [4, TT])
            ex = [A.alloc(F32, [TT]) for _ in range(4)]
            for h in range(4):
                e1 = ex[h % 2]
                e2 = ex[2 + h % 2]
                for ti, (t0, tn) in enumerate(TILES):
                    pb = bank()
                    P.mm(pb[:, 0:tn], w2t[0:16, l, h * 128:(h + 1) * 128], glrT[0:16, t0:t0 + tn], True, True)
                    P.I("act", "activation", out=e1[:, t0:t0 + tn], in_=pb[:, 0:tn], func=AF.Exp,
                        bias=nbg[:, l, h:h + 1], scale=-1.0)
                P.I("act", "activation", out=e2, in_=e1, func=AF.Ln, bias=1.0, scale=1.0)
                P.I("dve", "tensor_tensor_scan", out=Bc[:, h, :], data0=scanm, data1=e2,
                    initial=0.0, op0=ALU.mult, op1=ALU.add)
                P.I("act", "activation", out=decay[:, h, 0:8],
                    in_=Bc[:, h, 0:T].rearrange("p (c k) -> p c k", k=128)[:, :, 127], func=AF.Exp, scale=-1.0 / 16)
                P.I("act", "activation", out=decay[:, h, 8:9], in_=Bc[:, h, TT - 1:TT], func=AF.Exp, scale=-1.0 / 16)
            w = load_w_cols(win, 0, 512)

            def pre_qg(e):
                P.I("act", "activation", out=ex[e % 2], in_=Bc[:, e, :], func=AF.Exp, scale=-1.0 / 16)

            def ev_qg(e, ti, t0, tn, ps):
                P.I("dve", "scalar_tensor_tensor", out=qi[:, e, t0:t0 + tn], in0=ps, scalar=float(128 ** -0.5),
                    in1=ex[e % 2][:, t0:t0 + tn], op0=ALU.mult, op1=ALU.mult)
            proj_fm(w, 512, ev_qg, pre=pre_qg)
            w = load_w_cols(win, 512, 512)

            def pre_kg(e):
                ebi = ex[e % 2]
                ebe = ex[2 + e % 2]
                P.I("act", "activation", out=ebi, in_=Bc[:, e, :], func=AF.Exp, scale=1.0 / 16)
                P.I("dve", "tensor_tensor", out=ebe[:, 0:T].rearrange("p (c k) -> p c k", k=128),
                    in0=Bc[:, e, 0:T].rearrange("p (c k) -> p c k", k=128)[:, :, 127:128].to_broadcast([128, 8, 128]),
                    in1=Bc[:, e, 0:T].rearrange("p (c k) -> p c k", k=128), op=ALU.subtract)
                P.I("dve", "tensor_tensor", out=ebe[:, T:TT], in0=Bc[:, e, TT - 1:TT].to_broadcast([128, S]),
                    in1=Bc[:, e, T:TT], op=ALU.subtract)
                P.I("act", "activation", out=ebe, in_=ebe, func=AF.Exp, scale=-1.0 / 16)

            def ev_kg(e, ti, t0, tn, ps):
                P.I("dve", "tensor_tensor", out=ki[:, e, t0:t0 + tn], in0=ps, in1=ex[e % 2][:, t0:t0 + tn], op=ALU.mult)
                P.I("dve", "tensor_tensor", out=keT[:, e, t0:t0 + tn], in0=ps, in1=ex[2 + e % 2][:, t0:t0 + tn], op=ALU.mult)
            proj_fm(w, 512, ev_kg, pre=pre_kg)
            for bi, (b0, bn) in enumerate(BLKS):
                pb = bank_bf(bank())
                for h in range(4):
                    P.tr(pb[0:bn, h * 128:(h + 1) * 128], keT[:, h, b0:b0 + bn], ident_b)
                P.I("act", "activation", out=ke_tok[0:bn, bi, :], in_=pb[0:bn, 0:512], func=AF.Copy)
            A.release(mG)
            Vg = A.alloc(BF16, [9, 1024])
            for half in range(2):
                w = load_w_cols(win, 1024 + half * 512, 512)

                def ev_vg(bi, b0, bn, ps, half=half):
                    P.I("act", "activation", out=Vg[0:bn, bi, half * 512:(half + 1) * 512], in_=ps, func=AF.Copy)
                proj_tm(w, 512, ev_vg)

            chk(3)
            def gla_block(bi, Sst, emit_out, ogT):
                b0, bn = BLKS[bi]
                if emit_out:
                    for h in range(4):
                        pa = bank()
                        P.mm(pa[0:bn, 0:bn], ki[:, h, b0:b0 + bn], qi[:, h, b0:b0 + bn], True, True)
                        P.I("dve", "tensor_tensor", out=gla_att[h][0:bn, 0:bn], in0=pa[0:bn, 0:bn],
                            in1=gmask[0:bn, 0:bn], op=ALU.mult)
                for h in range(4):
                    if emit_out:
                        attm = gla_att[h]
                        po = [bank(), bank()]
                        sbt = Sb[bi % 2]
                        P.I("act", "activation", out=sbt[:, h, :], in_=Sst[:, h, :], func=AF.Copy)
                        for vc in range(2):
                            P.mm(po[vc][:, 0:bn], Vg[0:bn, bi, h * 256 + vc * 128:h * 256 + (vc + 1) * 128],
                                 attm[0:bn, 0:bn], True, False)
                        for vc in range(2):
                            P.mm(po[vc][:, 0:bn], sbt[:, h, vc * 128:(vc + 1) * 128],
                                 qi[:, h, b0:b0 + bn], False, True)
                    pk = bank()
                    P.mm(pk[:, 0:256], ke_tok[0:bn, bi, h * 128:(h + 1) * 128],
                         Vg[0:bn, bi, h * 256:(h + 1) * 256], True, True)
                    P.I("dve", "scalar_tensor_tensor", out=Sst[:, h, :], in0=Sst[:, h, :],
                        scalar=decay[:, h, bi:bi + 1], in1=pk[:, 0:256], op0=ALU.mult, op1=ALU.add)
                    if emit_out:
                        for vc in range(2):
                            P.I("act", "activation", out=ogT[:, h * 2 + vc, b0:b0 + bn], in_=po[vc][:, 0:bn],
                                func=AF.Copy)

            P.I("dve", "memset", ap=S32, constant=0.0)
            for bi in range(8):
                gla_block(bi, S32, False, None)
            P.dma("sp", cin_f[l][:, 0:1024].rearrange("p (h v) -> p h v", h=4), S32, dwrites=[("cin_f", l, 0)])
            P.cc([cin_f[l]], [cout_f[l]], GROUPS, dreads=[("cin_f", l, 0), ("cin_f", l, 1)], dwrites=[("cout_f", l)])
            for half in range(2):
                w = load_w_cols(win, 3088 + half * 512, 512)

                def ev_q(e, ti, t0, tn, ps, half=half):
                    h = half * 4 + e
                    if ti == 0:
                        bctr[0] += 1
                    st = bstage[bctr[0] % 2]
                    P.I("act", "activation", out=st[:, t0:t0 + tn], in_=ps, func=AF.Copy, scale=float(128 ** -0.5))
                    if ti == 2:
                        P.dma("sp", qscr[:, h, :], st, dwrites=[("qscr", h)])
                proj_fm(w, 512, ev_q)

            for half in range(2):
                w = load_w_cols(win, 2064 + half * 512, 512)

                def ev_rg(e, ti, t0, tn, ps, half=half):
                    c = half * 4 + e
                    if ti == 0:
                        bctr[0] += 1
                    st = bstage[bctr[0] % 2]
                    P.I("act", "activation", out=st[:, t0:t0 + tn], in_=ps, func=AF.Silu)
                    if ti == 2:
                        P.dma("sp", gscr[:, c, :], st, dwrites=[("gscr", c)])
                proj_fm(w, 512, ev_rg)

            P.dma("sp", S32s, sg[l].rearrange("h k v -> k h v"))
            mO = A.mark()
            ogT = A.alloc(F32, [8, TT])
            gla_att = [A.alloc(BF16, [128]) for _ in range(4)]
            gla_block(8, S32s, True, ogT)
            P.dma("sp", sso[l].rearrange("h k v -> k h v"), S32s)
            P.dma("sp", S32, cout_f[l][0:128, 0:1024].rearrange("p (h v) -> p h v", h=4), dreads=[("cout_f", l)])
            P.I("dve", "tensor_scalar", out=S32, in0=S32, scalar1=flg[:, 0:1], scalar2=None, op0=ALU.mult)
            for bi in range(8):
                gla_block(bi, S32, True, ogT)
            P.dma("sp", spo[l].rearrange("h k v -> k h v"), S32)
            osq = A.alloc(F32, [2, 512])
            orstd = A.alloc(F32, [512])
            gt2 = [A.alloc(BF16, [2, TT]) for _ in range(2)]
            for h in range(4):
                gg = gt2[h % 2]
                P.dma("sp", gg, gscr[:, 2 * h:2 * h + 2, :], dreads=[("gscr", 2 * h), ("gscr", 2 * h + 1)])
                for ti, (t0, tn) in enumerate(TILES):
                    P.I("act", "activation", out=osq[:, :, 0:tn], in_=ogT[:, 2 * h:2 * h + 2, t0:t0 + tn], func=AF.Square)
                    rms_rstd(lambda c: osq[:, c, 0:tn], tn, onesV, 2, orstd[:, 0:tn])
                    for vc in range(2):
                        c = 2 * h + vc
                        P.I("dve", "tensor_tensor", out=osq[:, vc, 0:tn], in0=ogT[:, c, t0:t0 + tn],
                            in1=orstd[:, 0:tn], op=ALU.mult)
                        P.I("dve", "scalar_tensor_tensor", out=act[:, c, t0:t0 + tn], in0=osq[:, vc, 0:tn],
                            scalar=g_on[:, l, c:c + 1], in1=gg[:, vc, t0:t0 + tn], op0=ALU.mult, op1=ALU.mult)
            A.release(mO)
            A.top = ll_top[0]

            chk(4)
            mF = A.mark()
            Rp = A.alloc(F32, [8, 8])
            P.dma("sp", Rp, cout_f[l][0:128, 1024:1088].rearrange("p (b h) -> p b h", b=8), dreads=[("cout_f", l)])
            P.I("dve", "tensor_scalar", out=Rp, in0=Rp, scalar1=negbig[:, 0:1], scalar2=None, op0=ALU.add)
            CL = A.alloc(F32, [16, 8])
            P.dma("sp", CL, clf[l].rearrange("(b p) h -> p b h", p=128))
            Rc = A.alloc(F32, [16, 8])
            Wc = A.alloc(F32, [16, 8])
            Tot = A.alloc(F32, [16, 8])
            Suf = A.alloc(F32, [16, 8])
            pb = bank()
            tri = A.alloc(F32, [128])
            P.I("dve", "tensor_scalar", out=tri, in0=dmask, scalar1=1.0 / (-NEG), scalar2=1.0, op0=ALU.mult, op1=ALU.add)
            P.mm(pb[:, 0:128], tri, CL.rearrange("p b h -> p (b h)"), True, True)
            P.I("act", "activation", out=Wc.rearrange("p b h -> p (b h)"), in_=pb[:, 0:128], func=AF.Copy)
            pb2 = bank()
            P.mm(pb2[:, 0:128], ones_f, CL.rearrange("p b h -> p (b h)"), True, True)
            P.I("act", "activation", out=Tot.rearrange("p b h -> p (b h)"), in_=pb2[:, 0:128], func=AF.Copy)
            P.I("dve", "memset", ap=Suf[:, 15, :], constant=0.0)
            for b in range(14, -1, -1):
                P.I("dve", "tensor_tensor", out=Suf[:, b, :], in0=Suf[:, b + 1, :], in1=Tot[:, b + 1, :], op=ALU.add)
            P.I("dve", "tensor_tensor", out=Rc, in0=Tot, in1=Wc, op=ALU.subtract)
            P.I("dve", "tensor_tensor", out=Rc, in0=Rc, in1=Suf, op=ALU.add)

            Bq = [A.alloc(F32, [TT]) for _ in range(2)]
            QTh = [A.alloc(BF16, [TT]) for _ in range(2)]
            selh = [A.alloc(F32, [128]) for _ in range(2)]
            KTo = [A.alloc(BF16, [T]) for _ in range(2)]
            Vo = [A.alloc(BF16, [8, 128]) for _ in range(2)]
            KTp = [A.alloc(BF16, [T]) for _ in range(2)]
            Vp = [A.alloc(BF16, [8, 128]) for _ in range(2)]
            Kc = [A.alloc(BF16, [16, 128]) for _ in range(2)]
            KTc = [A.alloc(BF16, [2048]) for _ in range(2)]
            Vc = [A.alloc(BF16, [16, 128]) for _ in range(2)]
            tmpS = [A.alloc(F32, [512]) for _ in range(6)]
            Pt = [A.alloc(BF16, [512]) for _ in range(6)]
            rden = A.alloc(F32, [512])
            kctr = [0]
            hcur = [0]

            NB = 6
            SKEW = 4

            def fox_S(lhsK, Vl, biascol, qcols, nq, pO, pD, first, last, diag, kn=128):
                i = kctr[0] % NB
                kctr[0] += 1
                h = hcur[0]
                pS = bank6()
                P.mm(pS[0:kn, 0:nq], lhsK, QTh[h % 2][:, qcols:qcols + nq], True, True)
                tS = tmpS[i]
                P.I("dve", "tensor_tensor", out=tS[0:kn, 0:nq], in0=pS[0:kn, 0:nq],
                    in1=Bq[h % 2][0:kn, qcols:qcols + nq], op=ALU.add)
                if diag:
                    dn = min(kn, nq)
                    P.I("dve", "tensor_tensor", out=tS[0:kn, 0:dn], in0=tS[0:kn, 0:dn], in1=dmask[0:kn, 0:dn], op=ALU.add)
                pt = Pt[i]
                P.I("act", "activation", out=pt[0:kn, 0:nq], in_=tS[0:kn, 0:nq], func=AF.Exp, bias=biascol, scale=1.0)
                return (pO, pD, Vl, pt[0:kn, 0:nq], first, last, kn)

            def fox_PV(rec):
                pO, pD, Vl, pt, first, last, kn = rec
                P.mm(pO, Vl, pt, first, last)
                P.mm(pD, ones_b[0:kn, :], pt, first, last)

            def fox_run(blocks):
                recs = []
                n = len(blocks)
                for i in range(n + SKEW):
                    if i < n:
                        recs.append(fox_S(*blocks[i][0], **blocks[i][1]))
                    if i >= SKEW:
                        fox_PV(recs[i - SKEW])

            coutV = cout_v[l][0:128, :].rearrange("p (b e) -> p b e", b=8)
            for h in range(8):
                hcur[0] = h
                bq = Bq[h % 2]
                sh = selh[h % 2]
                P.I("dve", "tensor_copy", out=sh[0:8, :], in_=ident_f[0:8, h:h + 1].to_broadcast([8, 128]))
                P.dma("sp", QTh[h % 2], qscr[:, h, :], dreads=[("qscr", h)])
                for ti, (t0, tn) in enumerate(TILES):
                    pb = bank()
                    P.mm(pb[:, 0:tn], sh[0:8, :], cT[0:8, t0:t0 + tn], True, True)
                    P.I("act", "activation", out=bq[:, t0:t0 + tn], in_=pb[:, 0:tn], func=AF.Copy)
                kto = KTo[h % 2]; vo = Vo[h % 2]
                ktp = KTp[h % 2]; vpp = Vp[h % 2]; kc = Kc[h % 2]; ktc = KTc[h % 2]; vc_ = Vc[h % 2]
                P.dma("sp", kto, cin_k[l][:, h * 1024:(h + 1) * 1024], dreads=[("cin_kv", l, "k", h)])
                P.dma("sp", vo, cinV[:, :, h * 128:(h + 1) * 128],
                      dreads=[("cin_kv", l, "v", b, h // 4) for b in range(8)])
                P.dma("sp", ktp, cout_k[l][0:128, h * 1024:(h + 1) * 1024], dreads=[("cout_k", l)])
                P.dma("sp", vpp, coutV[:, :, h * 128:(h + 1) * 128], dreads=[("cout_v", l)])
                for q4 in range(4):
                    P.dma("pool", kc[:, q4 * 4:(q4 + 1) * 4, :],
                          ck[l].rearrange("(b p) e -> p b e", p=128)[:, q4 * 4:(q4 + 1) * 4, h * 128:(h + 1) * 128])
                    P.dma("pool", vc_[:, q4 * 4:(q4 + 1) * 4, :],
                          cv[l].rearrange("(b p) e -> p b e", p=128)[:, q4 * 4:(q4 + 1) * 4, h * 128:(h + 1) * 128])
                for g4 in range(4):
                    pbb = bank_bf(bank())
                    for k in range(4):
                        P.tr(pbb[:, k * 128:(k + 1) * 128], kc[:, g4 * 4 + k, :], ident_b)
                    P.I("act", "activation", out=ktc[:, g4 * 512:(g4 + 1) * 512], in_=pbb[:, 0:512], func=AF.Copy)
                for qt in range(2):
                    q0 = qt * 512
                    pO = pst[:, 6, :]; pD = pst[:, 7, :]
                    nown = 4 * qt + 4
                    blocks = []
                    for kb in range(8):
                        blocks.append(((ktp[:, kb * 128:(kb + 1) * 128], vpp[:, kb, :], Rp[:, kb, h:h + 1],
                                        q0, 512, pO[:, 0:512], pD[:, 0:512], kb == 0, False, False), {}))
                    for kb in range(nown):
                        lo = max(q0, kb * 128)
                        nq = q0 + 512 - lo
                        blocks.append(((kto[:, kb * 128:(kb + 1) * 128], vo[:, kb, :],
                                        negc[:, kb, h:h + 1], lo, nq, pO[:, lo - q0:512], pD[:, lo - q0:512],
                                        False, kb == nown - 1, kb * 128 >= q0), {}))
                    fox_run(blocks)
                    P.I("dve", "reciprocal", out=rden, in_=pD[:, 0:512])
                    P.I("dve", "tensor_tensor", out=act[:, 8 + h, q0:q0 + 512], in0=pO[:, 0:512], in1=rden, op=ALU.mult)
                pO = pst[:, 6, :]; pD = pst[:, 7, :]
                blocks = []
                for kb in range(16):
                    blocks.append(((ktc[:, kb * 128:(kb + 1) * 128], vc_[:, kb, :], Rc[:, kb, h:h + 1],
                                    T, S, pO[:, 0:S], pD[:, 0:S], kb == 0, False, False), {}))
                blocks.append(((KTs[:, h, :], Vs[0:S, h * 128:(h + 1) * 128], negc[0:S, 8, h:h + 1],
                                T, S, pO[:, 0:S], pD[:, 0:S], False, True, True), {"kn": S}))
                fox_run(blocks)
                P.I("dve", "reciprocal", out=rden[:, 0:S], in_=pD[:, 0:S])
                P.I("dve", "tensor_tensor", out=act[:, 8 + h, T:TT], in0=pO[:, 0:S], in1=rden[:, 0:S], op=ALU.mult)
            A.release(mF)
            A.release(mL)

            chk(5)
            mW = A.mark()
            yT = A.alloc(F32, [16, TT])
            wo = w_out[l]
            for q4 in range(4):
                w = load_w_cols(wo, q4 * 512, 512)

                def ev_o(e, ti, t0, tn, ps, q4=q4):
                    P.I("act", "activation", out=yT[:, q4 * 4 + e, t0:t0 + tn], in_=ps, func=AF.Copy)
                proj_fm(w, 512, ev_o)
            postnorm_residual(yT, g_post, l, False, nxt=(g_mlp, l))
            chk(6)

            mU = A.mark()
            u = [A.alloc(BF16, [4, TT]) for _ in range(2)]
            rtmp = [A.alloc(F32, [512]) for _ in range(2)]
            rctr = 0
            for f in range(16):
                wu = load_w_cols(w_up[l], f * 512, 512)
                wd = load_w_rows(w_dn[l], f * 512, 4, 2048)
                uf = u[f % 2]

                def ev_up(e, ti, t0, tn, ps, uf=uf):
                    nonlocal rctr
                    r = rtmp[rctr % 2]
                    rctr += 1
                    P.I("act", "activation", out=r[:, 0:tn], in_=ps, func=AF.Relu)
                    P.I("dve", "tensor_tensor", out=uf[:, e, t0:t0 + tn], in0=r[:, 0:tn], in1=r[:, 0:tn], op=ALU.mult)
                proj_fm(wu, 512, ev_up)
                for dc in range(16):
                    banks = [bank() for _ in MTILES]
                    for e in range(4):
                        for ti, (t0, tn) in enumerate(MTILES):
                            P.mm(banks[ti][:, 0:tn], wd[:, e, dc * 128:(dc + 1) * 128], uf[:, e, t0:t0 + tn], e == 0, e == 3)
                    for ti, (t0, tn) in enumerate(MTILES):
                        if f == 0:
                            P.I("act", "activation", out=yT[:, dc, t0:t0 + tn], in_=banks[ti][:, 0:tn], func=AF.Copy)
                        else:
                            P.I("dve", "tensor_tensor", out=yT[:, dc, t0:t0 + tn], in0=yT[:, dc, t0:t0 + tn],
                                in1=banks[ti][:, 0:tn], op=ALU.add)
            A.release(mU)
            postnorm_residual(yT, g_mpost, l, final_layer, nxt=None if final_layer else (g_pre, l + 1))
            A.release(mW)

    try:
        layers()
    except _Stop:
        pass
    P.emit()
    es.close()
    print("SBUF arena peak bytes", A.peak, "ops", len(P.ops))
    return nc


_NC_CACHE = {}


def kernel(x_prompt, x_sample, cache_fox_k, cache_fox_v, cache_fox_logf, state_gla,
           g_mix_pre, w_in, w_gla_gate_up, b_gla_gate, b_fox_f, g_gla_onorm, w_out,
           g_mix_post, g_mlp_pre, w_mlp_up, w_mlp_down, g_mlp_post):
    f = lambda a: np.ascontiguousarray(np.asarray(a, dtype=np.float32))
    if "nc" not in _NC_CACHE:
        _NC_CACHE["nc"] = build()
    nc = _NC_CACHE["nc"]
    cst = make_consts()
    shared = dict(cst=cst, gmp=f(g_mix_pre), w_in=f(w_in), w2=f(w_gla_gate_up), bg=f(b_gla_gate),
                  bfx=f(b_fox_f), gon=f(g_gla_onorm), w_out=f(w_out), gpost=f(g_mix_post), gmlp=f(g_mlp_pre),
                  w_up=f(w_mlp_up), w_dn=f(w_mlp_down), gmpost=f(g_mlp_post))
    x_prompt = np.asarray(x_prompt); x_sample = np.asarray(x_sample)
    cache_fox_k = np.asarray(cache_fox_k); cache_fox_v = np.asarray(cache_fox_v)
    cache_fox_logf = np.asarray(cache_fox_logf); state_gla = np.asarray(state_gla)
    in_maps = []
    for c in range(8):
        b, half = c // 2, c % 2
        m = dict(shared)
        m["xp"] = f(x_prompt[b, half * T:(half + 1) * T])
        m["xs"] = f(x_sample[c])
        m["ck"] = f(cache_fox_k[:, c].reshape(NL, 2048, 1024))
        m["cv"] = f(cache_fox_v[:, c].reshape(NL, 2048, 1024))
        m["clf"] = f(cache_fox_logf[:, c])
        m["sg"] = f(state_gla[:, c])
        m["flag"] = np.full((128, 1), float(half), np.float32)
        in_maps.append(m)
    res = run_bass_kernel_spmd(nc, in_maps, core_ids=list(range(8)))
    R = res.results
    y_p = np.zeros((4, 2048, D), np.float32)
    k_p = np.zeros((NL, 4, 2048, 8, 128), np.float32)
    v_p = np.zeros((NL, 4, 2048, 8, 128), np.float32)
    f_p = np.zeros((NL, 4, 2048, 8), np.float32)
    s_p = np.zeros((NL, 4, 4, 128, 256), np.float32)
    y_s = np.zeros((8, S, D), np.float32)
    k_s = np.zeros((NL, 8, S, 8, 128), np.float32)
    v_s = np.zeros((NL, 8, S, 8, 128), np.float32)
    f_s = np.zeros((NL, 8, S, 8), np.float32)
    s_s = np.zeros((NL, 8, 4, 128, 256), np.float32)
    for c in range(8):
        b, half = c // 2, c % 2
        r = R[c]
        sl = slice(half * T, (half + 1) * T)
        y_p[b, sl] = r["yp"]
        k_p[:, b, sl] = r["kp"].reshape(NL, T, 8, 128)
        v_p[:, b, sl] = r["vp"].reshape(NL, T, 8, 128)
        f_p[:, b, sl] = r["fp"]
        if half == 1:
            s_p[:, b] = r["spo"]
        y_s[c] = r["ys"]
        k_s[:, c] = r["kso"].reshape(NL, S, 8, 128)
        v_s[:, c] = r["vso"].reshape(NL, S, 8, 128)
        f_s[:, c] = r["fso"]
        s_s[:, c] = r["sso"]
    return (y_p, y_s, k_p, v_p, f_p, s_p, k_s, v_s, f_s, s_s)
```

```python
import numpy as np
import concourse.bass as bass
import concourse.mybir as mybir
from concourse.bass_utils import run_bass_kernel_spmd

F32 = mybir.dt.float32
BF16 = mybir.dt.bfloat16
U8 = mybir.dt.uint8
AF = mybir.ActivationFunctionType
ALU = mybir.AluOpType

D = 2048
T = 1024
S = 32
TT = T + S
NL = 2
PW = 6168
DFF = 8192
TILES = [(0, 512), (512, 512), (1024, 32)]
MTILES = [(0, 352), (352, 352), (704, 352)]
BLKS = [(i * 128, 128) for i in range(8)] + [(1024, 32)]
NT256 = BLKS
EPS = 1e-6
NEG = -1.0e30
SAME_ENGINE_SYNC = True
STOP = 0


class _Stop(Exception):
    pass


DUMP = 0
_DUMP_FN = [None]


def chk(n):
    if DUMP == n:
        _DUMP_FN[0](n)
        raise _Stop()
    if STOP == n:
        raise _Stop()

_DSZ = {F32: 4, BF16: 2, U8: 1}


def _is_ap(v):
    return isinstance(v, bass.AP)


def _space(ap):
    n = type(ap.tensor).__name__
    if n.startswith("SB"):
        return "sb"
    if n.startswith("PS"):
        return "ps"
    return "dram"


CELL = 64


def _cells(ap):
    es = _DSZ[ap.dtype]
    dims = list(ap.ap)
    pstep = dims[0][0]
    off = int(ap.offset) % pstep if pstep else int(ap.offset)
    free = [(s, c) for (s, c) in dims[1:] if c > 1 and s != 0]
    free.sort(key=lambda x: -abs(x[0]))
    if free and free[-1][0] == 1:
        run = free[-1][1]
        outer = free[:-1]
    else:
        run = 1
        outer = free
    nout = 1
    for s, c in outer:
        nout *= c
    res = set()
    if nout > 256:
        ext = sum((c - 1) * s for s, c in free) + 1
        res.update(range((off * es) // CELL, ((off + ext) * es - 1) // CELL + 1))
        return res
    starts = [off]
    for s, c in outer:
        starts = [b + i * s for b in starts for i in range(c)]
    for b in starts:
        res.update(range((b * es) // CELL, ((b + run) * es - 1) // CELL + 1))
    return res


class Op:
    __slots__ = ("eng", "fn", "deps", "dma", "sig", "sem", "val", "cc")

    def __init__(self, eng, fn, dma, cc=False):
        self.eng = eng
        self.fn = fn
        self.deps = set()
        self.dma = dma
        self.sig = False
        self.sem = None
        self.val = 0
        self.cc = cc


class Prog:
    ENGS = ("pe", "act", "dve", "pool", "sp")

    def __init__(self, nc):
        self.nc = nc
        self.ops = []
        self.state = {"sb": {}, "ps": {}}
        self.dstate = {}

    def _touch(self, oid, reads, writes, dreads, dwrites):
        op = self.ops[oid]
        deps = op.deps
        for ap in reads:
            sp_ = _space(ap)
            st = self.state[sp_]
            if sp_ == "ps":
                for c in {c * CELL // 2048 for c in _cells(ap)}:
                    e = st.get(c)
                    if e is None:
                        st[c] = [None, [oid]]
                    else:
                        if e[0] is not None:
                            deps.add(e[0])
                        for r in e[1]:
                            if self.ops[r].eng != op.eng:
                                deps.add(r)
                        e[1].append(oid)
                continue
            for c in _cells(ap):
                e = st.get(c)
                if e is None:
                    st[c] = [None, [oid]]
                else:
                    if e[0] is not None:
                        deps.add(e[0])
                    e[1].append(oid)
        for ap in writes:
            sp_ = _space(ap)
            st = self.state[sp_]
            cl = _cells(ap)
            if sp_ == "ps":
                cl = {c * CELL // 2048 for c in cl}
            for c in cl:
                e = st.get(c)
                if e is None:
                    st[c] = [oid, []]
                else:
                    if e[0] is not None:
                        deps.add(e[0])
                    deps.update(e[1])
                    e[0] = oid
                    e[1] = []
        for k in dreads:
            e = self.dstate.setdefault(k, [None, []])
            if e[0] is not None:
                deps.add(e[0])
            e[1].append(oid)
        for k in dwrites:
            e = self.dstate.setdefault(k, [None, []])
            if e[0] is not None:
                deps.add(e[0])
            deps.update(e[1])
            e[0] = oid
            e[1] = []
        deps.discard(oid)

    def add(self, eng, fn, reads=(), writes=(), dreads=(), dwrites=(), dma=False, cc=False):
        oid = len(self.ops)
        self.ops.append(Op(eng, fn, dma, cc))
        self._touch(oid, reads, writes, dreads, dwrites)
        return oid

    def I(self, eng, meth, **kw):
        writes = [v for k, v in kw.items() if k in ("out", "accum_out", "ap") and _is_ap(v)]
        reads = [v for k, v in kw.items() if k not in ("out", "accum_out", "ap") and _is_ap(v)]
        reads = [r for r in reads if _space(r) != "dram"]
        return self.add(eng, lambda e: getattr(e, meth)(**kw), reads, writes)

    def mm(self, out, lhsT, rhs, start, stop):
        return self.add("pe", lambda e: e.matmul(out, lhsT, rhs, start=start, stop=stop),
                        [lhsT, rhs], [out])

    def tr(self, out, in_, ident):
        return self.add("pe", lambda e: e.transpose(out, in_, ident), [in_, ident], [out])

    def dma(self, q, out, in_, dreads=(), dwrites=(), noncontig=False):
        reads = [in_] if _space(in_) != "dram" else []
        writes = [out] if _space(out) != "dram" else []
        nc = self.nc
        if noncontig:
            def fn(e):
                with nc.allow_non_contiguous_dma(reason="small strided parameter load"):
                    return e.dma_start(out=out, in_=in_)
        else:
            def fn(e):
                return e.dma_start(out=out, in_=in_)
        return self.add(q, fn, reads, writes, dreads, dwrites, dma=True)

    def cc(self, ins, outs, groups, dreads, dwrites):
        import os
        if os.environ.get('KNOCC'):
            return None
        def fn(e):
            return e.collective_compute("AllGather", ALU.bypass, replica_groups=groups,
                                        ins=ins, outs=outs)
        return self.add("pool", fn, [], [], dreads, dwrites, dma=True, cc=True)

    def emit(self, out_final_keys=()):
        nc = self.nc
        ops = self.ops
        NS = 20
        for o in ops:
            for d in o.deps:
                p = ops[d]
                if p.dma:
                    continue
                if p.eng == o.eng and not o.dma:
                    if p.eng == "pe" or not SAME_ENGINE_SYNC:
                        continue
                p.sig = True
        import contextlib
        with contextlib.ExitStack() as es:
            esem = {e: es.enter_context(nc.semaphore("s_" + e)) for e in ("pe", "act", "dve", "pool")}
            dsem = {q: [es.enter_context(nc.semaphore("d_%s_%d" % (q, i))) for i in range(NS)]
                    for q in ("sp", "pool", "act")}
            ncc = sum(1 for o in ops if o.cc)
            csem = [es.enter_context(nc.semaphore("cc_%d" % i)) for i in range(ncc)]
            cnt = {e: 0 for e in esem}
            dcnt = {q: 0 for q in dsem}
            ci = 0
            prewait = {}
            for i, o in enumerate(ops):
                if o.cc:
                    o.sem = csem[ci]
                    o.val = 1
                    ci += 1
                elif o.dma:
                    n = dcnt[o.eng]
                    dcnt[o.eng] += 1
                    o.sem = dsem[o.eng][n % NS]
                    o.val = 16 * (n // NS + 1)
                    if n >= NS:
                        prewait[i] = (o.sem, o.val - 16)
                elif o.sig:
                    cnt[o.eng] += 1
                    o.sem = esem[o.eng]
                    o.val = cnt[o.eng]
            per = {e: [] for e in self.ENGS}
            for i, o in enumerate(ops):
                per[o.eng].append(i)
            finals = {}
            for o in ops:
                if o.dma:
                    k = id(o.sem)
                    if k not in finals or finals[k][1] < o.val:
                        finals[k] = (o.sem, o.val)
            blk = es.enter_context(nc.Block())

            def run(engname, eng):
                waited = {}
                for i in per[engname]:
                    o = ops[i]
                    need = {}
                    if i in prewait:
                        s, v = prewait[i]
                        need[id(s)] = (s, v)
                    for d in o.deps:
                        p = ops[d]
                        if not p.dma:
                            if p.eng == o.eng and not o.dma and (p.eng == "pe" or not SAME_ENGINE_SYNC):
                                continue
                        k = id(p.sem)
                        if k not in need or need[k][1] < p.val:
                            need[k] = (p.sem, p.val)
                    for k, (s, v) in need.items():
                        if waited.get(k, 0) >= v:
                            continue
                        eng.wait_ge(s, v)
                        waited[k] = v
                    ins = o.fn(eng)
                    if o.cc:
                        ins.then_inc(o.sem, 1)
                    elif o.dma:
                        ins.then_inc(o.sem, 16)
                    elif o.sig:
                        ins.then_inc(o.sem, 1)
                if engname == "sp":
                    for k, (s, v) in finals.items():
                        if waited.get(k, 0) < v:
                            eng.wait_ge(s, v)

            @blk.sync
            def _(e):
                run("sp", e)

            @blk.gpsimd
            def _(e):
                run("pool", e)

            @blk.tensor
            def _(e):
                run("pe", e)

            @blk.scalar
            def _(e):
                run("act", e)

            @blk.vector
            def _(e):
                run("dve", e)


class Arena:
    def __init__(self, sb, nbytes):
        self.sb = sb
        self.n = nbytes
        self.top = 0
        self.peak = 0

    def alloc(self, dtype, shape, parts=128):
        es = _DSZ[dtype]
        n = es
        for s in shape:
            n *= s
        off = (self.top + CELL - 1) // CELL * CELL
        assert off + n <= self.n, ("SBUF arena overflow", off, n, self.n)
        self.top = off + n
        self.peak = max(self.peak, self.top)
        ap = self.sb[:, off:off + n].bitcast(dtype)
        if len(shape) == 2:
            ap = ap.rearrange("p (a b) -> p a b", a=shape[0])
        elif len(shape) == 3:
            ap = ap.rearrange("p (a b c) -> p a b c", a=shape[0], b=shape[1])
        return ap

    def mark(self):
        return self.top

    def release(self, m):
        self.top = m


C_ID = 0
C_GM = 128
C_DM = 256
C_ONE = 384
C_OD = 512
C_OV = 640
C_SC = 768
C_ON = 768 + TT
NCST = 768 + TT


def make_consts():
    c = np.zeros((128, NCST), np.float32)
    i = np.arange(128)
    c[:, C_ID:C_ID + 128] = np.eye(128, dtype=np.float32)
    s = i[:, None]
    q = i[None, :]
    c[:, C_GM:C_GM + 128] = (s <= q).astype(np.float32)
    c[:, C_DM:C_DM + 128] = np.where(s <= q, 0.0, NEG).astype(np.float32)
    c[:, C_ONE:C_ONE + 128] = 1.0
    c[:, C_OD:C_OD + 128] = 1.0 / 2048.0
    c[:, C_OV:C_OV + 128] = 1.0 / 256.0
    m = np.ones(TT, np.float32)
    m[::128] = 0.0
    c[:, C_SC:C_SC + TT] = m[None, :]
    return c


def make_sel():
    s = np.zeros((8, 8, 128), np.float32)
    for h in range(8):
        s[h, h, :] = 1.0
    return s.reshape(8, 1024)


def build(nlayers=NL):
    nc = bass.Bass("TRN2", target_bir_lowering=False)
    dt = nc.dram_tensor

    def din(name, shape):
        return dt(name, list(shape), F32, kind="ExternalInput").ap()

    def dout(name, shape):
        return dt(name, list(shape), F32, kind="ExternalOutput").ap()

    xp = din("xp", [T, D]); xs = din("xs", [S, D])
    ck = din("ck", [NL, 2048, 1024]); cv = din("cv", [NL, 2048, 1024])
    clf = din("clf", [NL, 2048, 8]); sg = din("sg", [NL, 4, 128, 256])
    flag = din("flag", [128, 1]); cst = din("cst", [128, NCST])
    gmp = din("gmp", [NL, D]); w_in = din("w_in", [NL, D, PW]); w2 = din("w2", [NL, 16, 512])
    bg = din("bg", [NL, 512]); bfx = din("bfx", [NL, 8]); gon = din("gon", [NL, 1024])
    w_out = din("w_out", [NL, D, D]); gpost = din("gpost", [NL, D]); gmlp = din("gmlp", [NL, D])
    w_up = din("w_up", [NL, D, DFF]); w_dn = din("w_dn", [NL, DFF, D]); gmpost = din("gmpost", [NL, D])
    yp = dout("yp", [T, D]); ys = dout("ys", [S, D])
    kp = dout("kp", [NL, T, 1024]); vp = dout("vp", [NL, T, 1024]); fp = dout("fp", [NL, T, 8])
    spo = dout("spo", [NL, 4, 128, 256])
    kso = dout("kso", [NL, S, 1024]); vso = dout("vso", [NL, S, 1024]); fso = dout("fso", [NL, S, 8])
    sso = dout("sso", [NL, 4, 128, 256])
    xres = dt("xres", [128, 16, TT], F32, kind="Internal").ap()
    cin_k = [dt("cin_k%d" % l, [128, 8192], BF16, kind="Internal").ap() for l in range(NL)]
    cout_k = [dt("cout_k%d" % l, [256, 8192], BF16, kind="Internal").ap() for l in range(NL)]
    cin_v = [dt("cin_v%d" % l, [128, 8192], BF16, kind="Internal").ap() for l in range(NL)]
    cout_v = [dt("cout_v%d" % l, [256, 8192], BF16, kind="Internal").ap() for l in range(NL)]
    NF = 1024 + 64
    cin_f = [dt("cin_f%d" % l, [128, NF], F32, kind="Internal").ap() for l in range(NL)]
    cout_f = [dt("cout_f%d" % l, [256, NF], F32, kind="Internal").ap() for l in range(NL)]
    GROUPS = [[0, 1], [2, 3], [4, 5], [6, 7]]
    qscr = dt("qscr", [128, 8, TT], BF16, kind="Internal").ap()
    gscr = dt("gscr", [128, 8, TT], BF16, kind="Internal").ap()

    import contextlib
    es = contextlib.ExitStack()
    ARENA_BYTES = 206 * 1024
    sb = es.enter_context(nc.sbuf_tensor("arena", [128, ARENA_BYTES], U8))
    pst = es.enter_context(nc.psum_tensor("pst", [128, 8, 512], F32))
    A = Arena(sb, ARENA_BYTES)
    P = Prog(nc)
    bank_ctr = [0]
    ll_top = [0]

    def bank():
        b = bank_ctr[0] % 8
        bank_ctr[0] += 1
        return pst[:, b, :]

    bank6_ctr = [0]

    def bank6():
        b = bank6_ctr[0] % 6
        bank6_ctr[0] += 1
        return pst[:, b, :]

    def bank_bf(b_ap):
        return b_ap.bitcast(BF16)

    cs = A.alloc(F32, [NCST])
    P.dma("sp", cs, cst)
    ident_f = cs[:, C_ID:C_ID + 128]
    gmask = cs[:, C_GM:C_GM + 128]
    dmask = cs[:, C_DM:C_DM + 128]
    ones_f = cs[:, C_ONE:C_ONE + 128]
    onesD = cs[:, C_OD:C_OD + 128]
    onesV = cs[:, C_OV:C_OV + 128]
    scanm = cs[:, C_SC:C_SC + TT]
    ident_b = A.alloc(BF16, [128])
    P.I("dve", "tensor_copy", out=ident_b, in_=ident_f)
    ones_b = A.alloc(BF16, [128])
    P.I("dve", "tensor_copy", out=ones_b, in_=ones_f)
    flg = A.alloc(F32, [1])
    P.dma("sp", flg, flag)
    epst = A.alloc(F32, [1])
    P.I("dve", "memset", ap=epst, constant=EPS)
    negbig = A.alloc(F32, [1])
    P.I("dve", "tensor_scalar", out=negbig, in0=flg, scalar1=-1.0, scalar2=1.0e30, op0=ALU.add, op1=ALU.mult)

    prow = A.alloc(F32, [128])

    def pvec(src, n):
        t = A.alloc(F32, [NL, n])
        r = NL * n
        P.dma("sp", prow[0:r, :], src.rearrange("l (c p) -> (l c) p", p=128))
        pbk = bank()
        P.tr(pbk[:, 0:r], prow[0:r, :], ident_f[0:r, 0:r])
        P.I("act", "activation", out=t.rearrange("p l c -> p (l c)"), in_=pbk[:, 0:r], func=AF.Copy)
        return t
    g_pre = pvec(gmp, 16); g_post = pvec(gpost, 16); g_mlp = pvec(gmlp, 16); g_mpost = pvec(gmpost, 16)
    g_on = pvec(gon, 8)
    nbg = pvec(bg, 4)
    P.I("dve", "tensor_scalar", out=nbg, in0=nbg, scalar1=-1.0, scalar2=None, op0=ALU.mult)
    nbf = A.alloc(F32, [NL])
    P.dma("sp", prow[0:NL, 0:8], bfx)
    pbk = bank()
    P.tr(pbk[0:8, 0:NL], prow[0:NL, 0:8], ident_f[0:NL, 0:NL])
    P.I("dve", "tensor_scalar", out=nbf[0:8, :], in0=pbk[0:8, 0:NL], scalar1=-1.0, scalar2=None, op0=ALU.mult)
    w2t = A.alloc(F32, [NL, 512])
    P.dma("sp", w2t[0:16, :, :], w2.rearrange("l r e -> r l e"))

    act = A.alloc(BF16, [16, TT])
    WSLOT = 2
    wbuf = [A.alloc(BF16, [8192]) for _ in range(WSLOT)]
    wctr = [0]

    def wslot():
        w = wbuf[wctr[0] % WSLOT]
        wctr[0] += 1
        return w

    def load_w_cols(src2d, c0, n):
        w = wslot()[:, 0:16 * n].rearrange("p (c n) -> p c n", c=16)
        sv = src2d[:, c0:c0 + n].rearrange("(c p) n -> p c n", p=128)
        for q in range(4):
            P.dma("pool", w[:, q * 4:(q + 1) * 4, :], sv[:, q * 4:(q + 1) * 4, :])
        return w

    def load_w_rows(src2d, r0, nchunks, ncols):
        w = wslot()[:, 0:nchunks * ncols].rearrange("p (c n) -> p c n", c=nchunks)
        sv = src2d[r0:r0 + nchunks * 128, :].rearrange("(c p) n -> p c n", p=128)
        for q in range(nchunks):
            P.dma("pool", w[:, q:q + 1, :], sv[:, q:q + 1, :])
        return w

    def proj_fm(w, ncol, evac, m_chunk=128, pre=None, tiles=MTILES):
        ne = (ncol + m_chunk - 1) // m_chunk
        for e in range(ne):
            m = min(m_chunk, ncol - e * m_chunk)
            if pre is not None:
                pre(e)
            banks = [bank() for _ in tiles]
            for d in range(16):
                for ti, (t0, tn) in enumerate(tiles):
                    P.mm(banks[ti][0:m, 0:tn], w[:, d, e * m_chunk:e * m_chunk + m], act[:, d, t0:t0 + tn],
                         d == 0, d == 15)
            for ti, (t0, tn) in enumerate(tiles):
                evac(e, ti, t0, tn, banks[ti][0:m, 0:tn])

    def proj_tm(w, ncol, evac):
        for bi, (b0, bn) in enumerate(BLKS):
            pb = bank()
            for d in range(16):
                P.mm(pb[0:bn, 0:ncol], act[:, d, b0:b0 + bn], w[:, d, 0:ncol], d == 0, d == 15)
            evac(bi, b0, bn, pb[0:bn, 0:ncol])

    def rms_rstd(src_sq_chunks, w, ones_m, nchunk, out_rstd):
        pb = bank()
        for c in range(nchunk):
            P.mm(pb[:, 0:w], ones_m, src_sq_chunks(c), c == 0, c == nchunk - 1)
        P.I("act", "activation", out=out_rstd, in_=pb[:, 0:w], func=AF.Sqrt, bias=epst[:, 0:1], scale=1.0)
        P.I("dve", "reciprocal", out=out_rstd, in_=out_rstd)

    def norm_from_xres(gt, l):
        m = A.mark()
        NX = 3
        xin = [A.alloc(F32, [16, 128]) for _ in range(NX)]
        sqs = [A.alloc(F32, [16, 128]) for _ in range(2)]
        rstds = [A.alloc(F32, [128]) for _ in range(2)]
        for i, (t0, tn) in enumerate(NT256):
            x = xin[i % NX]
            sq = sqs[i % 2]
            rstd = rstds[i % 2]
            P.dma("sp", x[:, :, 0:tn], xres[:, :, t0:t0 + tn], dreads=[("xres", i)])
            P.I("act", "activation", out=sq[:, :, 0:tn], in_=x[:, :, 0:tn], func=AF.Square)
            rms_rstd(lambda c: sq[:, c, 0:tn], tn, onesD, 16, rstd[:, 0:tn])
            for c in range(16):
                eng = "dve"
                P.I(eng, "scalar_tensor_tensor", out=act[:, c, t0:t0 + tn], in0=x[:, c, 0:tn],
                    scalar=gt[:, l, c:c + 1], in1=rstd[:, 0:tn], op0=ALU.mult, op1=ALU.mult)
        A.release(m)

    def postnorm_residual(yT, gt, l, final, nxt=None):
        m = A.mark()
        NX = 2 if final else 3
        NQ = 1 if final else 2
        xin = [A.alloc(F32, [16, 128]) for _ in range(NX)]
        sqs = [A.alloc(F32, [16, 128]) for _ in range(NQ)]
        rstds = [A.alloc(F32, [128]) for _ in range(2)]
        rstds2 = [A.alloc(F32, [128]) for _ in range(2)]
        stage = [A.alloc(F32, [2048]) for _ in range(2)] if final else None
        sctr = 0
        for i, (t0, tn) in enumerate(NT256):
            x = xin[i % NX]
            sq = sqs[i % NQ]
            rstd = rstds[i % 2]
            P.dma("sp", x[:, :, 0:tn], xres[:, :, t0:t0 + tn], dreads=[("xres", i)])
            P.I("act", "activation", out=sq[:, :, 0:tn], in_=yT[:, :, t0:t0 + tn], func=AF.Square)
            rms_rstd(lambda c: sq[:, c, 0:tn], tn, onesD, 16, rstd[:, 0:tn])
            for c in range(16):
                eng = "dve"
                P.I(eng, "scalar_tensor_tensor", out=sq[:, c, 0:tn], in0=yT[:, c, t0:t0 + tn],
                    scalar=gt[:, l, c:c + 1], in1=rstd[:, 0:tn], op0=ALU.mult, op1=ALU.mult)
            P.I("dve", "tensor_tensor", out=x[:, 0:10, 0:tn], in0=x[:, 0:10, 0:tn], in1=sq[:, 0:10, 0:tn], op=ALU.add)
            P.I("pool", "tensor_tensor", out=x[:, 10:16, 0:tn], in0=x[:, 10:16, 0:tn], in1=sq[:, 10:16, 0:tn], op=ALU.add)
            if nxt is not None:
                ngt, nl = nxt
                rstd2 = rstds2[i % 2]
                P.I("act", "activation", out=sq[:, :, 0:tn], in_=x[:, :, 0:tn], func=AF.Square)
                rms_rstd(lambda c: sq[:, c, 0:tn], tn, onesD, 16, rstd2[:, 0:tn])
                for c in range(16):
                    P.I("dve", "scalar_tensor_tensor", out=act[:, c, t0:t0 + tn], in0=x[:, c, 0:tn],
                        scalar=ngt[:, nl, c:c + 1], in1=rstd2[:, 0:tn], op0=ALU.mult, op1=ALU.mult)
            if not final:
                P.dma("sp", xres[:, :, t0:t0 + tn], x[:, :, 0:tn], dwrites=[("xres", i)])
            else:
                for sb0 in range(0, tn, 128):
                    bn = min(128, tn - sb0)
                    st = stage[sctr % 2]
                    sctr += 1
                    for c4 in range(4):
                        pb = bank()
                        for k in range(4):
                            c = c4 * 4 + k
                            P.tr(pb[0:bn, k * 128:(k + 1) * 128], x[:, c, sb0:sb0 + bn], ident_f)
                        P.I("act", "activation", out=st[0:bn, c4 * 512:(c4 + 1) * 512], in_=pb[0:bn, :], func=AF.Copy)
                    tok = t0 + sb0
                    if tok < T:
                        P.dma("sp", yp[tok:tok + bn, :], st[0:bn, :])
                    else:
                        P.dma("sp", ys[0:bn, :], st[0:bn, :])
        A.release(m)

    def dump_x(n):
        m = A.mark()
        xin = [A.alloc(F32, [16, 128]) for _ in range(2)]
        stage = [A.alloc(F32, [2048]) for _ in range(2)]
        for i, (t0, tn) in enumerate(BLKS):
            x = xin[i % 2]
            st = stage[i % 2]
            if n == 5:
                P.I("dve", "tensor_copy", out=x[:, :, 0:tn], in_=act[:, :, t0:t0 + tn])
            else:
                P.dma("sp", x[:, :, 0:tn], xres[:, :, t0:t0 + tn], dreads=[("xres", i)])
            for c4 in range(4):
                pb = bank()
                for k in range(4):
                    c = c4 * 4 + k
                    P.tr(pb[0:tn, k * 128:(k + 1) * 128], x[:, c, 0:tn], ident_f)
                P.I("act", "activation", out=st[0:tn, c4 * 512:(c4 + 1) * 512], in_=pb[0:tn, :], func=AF.Copy)
            if t0 < T:
                P.dma("sp", yp[t0:t0 + tn, :], st[0:tn, :])
            else:
                P.dma("sp", ys[0:tn, :], st[0:tn, :])
        A.release(m)
    _DUMP_FN[0] = dump_x

    m0 = A.mark()
    xtok = [A.alloc(F32, [2048]) for _ in range(2)]
    xst = [A.alloc(F32, [16, 128]) for _ in range(3)]
    sq0 = [A.alloc(F32, [16, 128]) for _ in range(2)]
    rs0 = [A.alloc(F32, [128]) for _ in range(2)]
    for bi, (b0, bn) in enumerate(BLKS):
        xt = xtok[bi % 2]
        st = xst[bi % 3]
        if b0 < T:
            P.dma("sp", xt[0:bn, :], xp[b0:b0 + bn, :])
        else:
            P.dma("sp", xt[0:bn, :], xs[0:bn, :])
        for c4 in range(4):
            pb = bank()
            for k in range(4):
                c = c4 * 4 + k
                P.tr(pb[:, k * 128:k * 128 + bn], xt[0:bn, c * 128:(c + 1) * 128], ident_f[0:bn, 0:bn])
            P.I("act", "activation", out=st[:, c4 * 4:(c4 + 1) * 4, 0:bn],
                in_=pb.rearrange("p (k n) -> p k n", k=4)[:, :, 0:bn], func=AF.Copy)
        P.dma("sp", xres[:, :, b0:b0 + bn], st[:, :, 0:bn], dwrites=[("xres", bi)])
        sq = sq0[bi % 2]
        rstd = rs0[bi % 2]
        P.I("act", "activation", out=sq[:, :, 0:bn], in_=st[:, :, 0:bn], func=AF.Square)
        rms_rstd(lambda c: sq[:, c, 0:bn], bn, onesD, 16, rstd[:, 0:bn])
        for c in range(16):
            P.I("dve", "scalar_tensor_tensor", out=act[:, c, b0:b0 + bn], in0=st[:, c, 0:bn],
                scalar=g_pre[:, 0, c:c + 1], in1=rstd[:, 0:bn], op0=ALU.mult, op1=ALU.mult)
    A.release(m0)

    def layers():
        for l in range(nlayers):
            final_layer = (l == nlayers - 1)
            chk(99)
            chk(98)
            mL = A.mark()
            win = w_in[l]
            KTs = A.alloc(BF16, [8, S])
            Vs = A.alloc(BF16, [1024])
            lfT = A.alloc(F32, [TT])
            cT = A.alloc(F32, [TT])
            lf_tok = A.alloc(F32, [9, 8])
            negc = A.alloc(F32, [9, 8])
            Rown = A.alloc(F32, [8, 8])
            decay = A.alloc(F32, [4, 9])
            S32 = A.alloc(F32, [4, 256])
            S32s = A.alloc(F32, [4, 256])
            Sb = [A.alloc(BF16, [4, 256]) for _ in range(2)]
            ostage = [A.alloc(F32, [512]) for _ in range(2)]
            bstage = [A.alloc(BF16, [TT]) for _ in range(2)]
            bctr = [0]
            ll_top[0] = A.top

            for half in range(2):
                w = load_w_cols(win, 4112 + half * 512, 512)

                def ev_k(e, ti, t0, tn, ps, half=half):
                    h = half * 4 + e
                    if ti == 0:
                        bctr[0] += 1
                    st = bstage[bctr[0] % 2]
                    if t0 < T:
                        P.I("act", "activation", out=st[:, t0:t0 + tn], in_=ps, func=AF.Copy)
                    else:
                        P.I("act", "activation", out=KTs[:, h, :], in_=ps, func=AF.Copy)
                        P.dma("sp", cin_k[l][:, h * 1024:(h + 1) * 1024], st[:, 0:T], dwrites=[("cin_kv", l, "k", h)])
                proj_fm(w, 512, ev_k, tiles=TILES)
                chk(97)

                def ev_ktm(bi, b0, bn, ps, half=half):
                    st = ostage[bi % 2]
                    P.I("act", "activation", out=st[0:bn, :], in_=ps, func=AF.Copy)
                    if b0 < T:
                        P.dma("sp", kp[l, b0:b0 + bn, half * 512:(half + 1) * 512], st[0:bn, :])
                    else:
                        P.dma("sp", kso[l, 0:bn, half * 512:(half + 1) * 512], st[0:bn, :])
                proj_tm(w, 512, ev_ktm)
                chk(96 - half)
            cinV = cin_v[l].rearrange("p (b e) -> p b e", b=8)
            for half in range(2):
                w = load_w_cols(win, 5136 + half * 512, 512)

                def ev_v(bi, b0, bn, ps, half=half):
                    st = ostage[bi % 2]
                    P.I("act", "activation", out=st[0:bn, :], in_=ps, func=AF.Copy)
                    if b0 < T:
                        bctr[0] += 1
                        sb_ = bstage[bctr[0] % 2]
                        P.I("dve", "tensor_copy", out=sb_[:, 0:512], in_=ps)
                        P.dma("sp", cinV[:, bi, half * 512:(half + 1) * 512], sb_[:, 0:512],
                              dwrites=[("cin_kv", l, "v", bi, half)])
                        P.dma("sp", vp[l, b0:b0 + bn, half * 512:(half + 1) * 512], st[0:bn, :])
                    else:
                        P.I("dve", "tensor_copy", out=Vs[0:bn, half * 512:(half + 1) * 512], in_=ps)
                        P.dma("sp", vso[l, 0:bn, half * 512:(half + 1) * 512], st[0:bn, :])
                proj_tm(w, 512, ev_v)
                chk(94 - half)
            P.cc([cin_k[l]], [cout_k[l]], GROUPS, dreads=[("cin_kv", l, "k", h) for h in range(8)],
                 dwrites=[("cout_k", l)])
            P.cc([cin_v[l]], [cout_v[l]], GROUPS,
                 dreads=[("cin_kv", l, "v", b, hf) for b in range(8) for hf in range(2)], dwrites=[("cout_v", l)])
            chk(1)

            wfl = wslot()[:, 0:128].rearrange("p (c n) -> p c n", c=16)
            P.dma("pool", wfl, win[:, 6160:6168].rearrange("(c p) n -> p c n", p=128))
            mt = A.mark()
            etmp = A.alloc(F32, [TT])

            def ev_fl(e, ti, t0, tn, ps):
                P.I("act", "activation", out=etmp[0:8, t0:t0 + tn], in_=ps, func=AF.Exp,
                    bias=nbf[0:8, l:l + 1], scale=-1.0)
            proj_fm(wfl, 8, ev_fl, m_chunk=8)
            P.I("act", "activation", out=lfT[0:8, :], in_=etmp[0:8, :], func=AF.Ln, bias=1.0, scale=1.0)
            P.I("dve", "tensor_scalar", out=lfT[0:8, :], in0=lfT[0:8, :], scalar1=-1.0, scalar2=None, op0=ALU.mult)
            onesr = A.alloc(F32, [T])
            P.I("dve", "memset", ap=onesr[0:8, :], constant=1.0)
            P.I("dve", "tensor_tensor_scan", out=cT[0:8, 0:T], data0=onesr[0:8, 0:T],
                data1=lfT[0:8, 0:T], initial=0.0, op0=ALU.mult, op1=ALU.add)
            P.I("dve", "tensor_tensor_scan", out=cT[0:8, T:TT], data0=onesr[0:8, 0:S],
                data1=lfT[0:8, T:TT], initial=0.0, op0=ALU.mult, op1=ALU.add)
            A.release(mt)
            for bi, (b0, bn) in enumerate(BLKS):
                pb = bank()
                P.tr(pb[0:bn, 0:8], lfT[0:8, b0:b0 + bn], ident_f[0:8, 0:8])
                P.tr(pb[0:bn, 8:16], cT[0:8, b0:b0 + bn], ident_f[0:8, 0:8])
                P.I("act", "activation", out=lf_tok[0:bn, bi, :], in_=pb[0:bn, 0:8], func=AF.Copy)
                P.I("dve", "tensor_scalar", out=negc[0:bn, bi, :], in0=pb[0:bn, 8:16], scalar1=-1.0, scalar2=None,
                    op0=ALU.mult)
            P.dma("sp", fp[l].rearrange("(b p) h -> p b h", p=128), lf_tok[:, 0:8, :])
            P.dma("sp", fso[l], lf_tok[0:S, 8, :])
            mt = A.mark()
            cl8 = A.alloc(F32, [128])
            P.I("dve", "tensor_copy", out=cl8[0:8, :], in_=cT[0:8, T - 1:T].to_broadcast([8, 128]))
            pb = bank()
            P.mm(pb[:, 0:8], cl8[0:8, :], ident_f[0:8, 0:8], True, True)
            ctb = A.alloc(F32, [8])
            P.I("act", "activation", out=ctb, in_=pb[:, 0:8], func=AF.Copy)
            for b in range(8):
                P.I("dve", "tensor_tensor", out=Rown[:, b, :], in0=negc[:, b, :], in1=ctb, op=ALU.add)
            A.release(mt)
            P.dma("sp", cin_f[l][:, 1024:1088].rearrange("p (b h) -> p b h", b=8), Rown,
                  dwrites=[("cin_f", l, 1)])

            chk(2)
            qi = A.alloc(BF16, [4, TT])
            ki = A.alloc(BF16, [4, TT])
            ke_tok = A.alloc(BF16, [9, 512])
            mG = A.mark()
            wgl = wslot()[:, 0:256].rearrange("p (c n) -> p c n", c=16)
            P.dma("pool", wgl, win[:, 2048:2064].rearrange("(c p) n -> p c n", p=128))
            glrT = A.alloc(F32, [TT])

            def ev_glr(e, ti, t0, tn, ps):
                P.I("act", "activation", out=glrT[0:16, t0:t0 + tn], in_=ps, func=AF.Copy)
            proj_fm(wgl, 16, ev_glr, m_chunk=16)
            Bc = A.alloc(F32, [4, TT])
            keT = A.alloc(BF16, [4, TT])
            ex = [A.alloc(F32, [TT]) for _ in range(4)]
            for h in range(4):
                e1 = ex[h % 2]
                e2 = ex[2 + h % 2]
                for ti, (t0, tn) in enumerate(TILES):
                    pb = bank()
                    P.mm(pb[:, 0:tn], w2t[0:16, l, h * 128:(h + 1) * 128], glrT[0:16, t0:t0 + tn], True, True)
                    P.I("act", "activation", out=e1[:, t0:t0 + tn], in_=pb[:, 0:tn], func=AF.Exp,
                        bias=nbg[:, l, h:h + 1], scale=-1.0)
                P.I("act", "activation", out=e2, in_=e1, func=AF.Ln, bias=1.0, scale=1.0)
                P.I("dve", "tensor_tensor_scan", out=Bc[:, h, :], data0=scanm, data1=e2,
                    initial=0.0, op0=ALU.mult, op1=ALU.add)
                P.I("act", "activation", out=decay[:, h, 0:8],
                    in_=Bc[:, h, 0:T].rearrange("p (c k) -> p c k", k=128)[:, :, 127], func=AF.Exp, scale=-1.0 / 16)
                P.I("act", "activation", out=decay[:, h, 8:9], in_=Bc[:, h, TT - 1:TT], func=AF.Exp, scale=-1.0 / 16)
            w = load_w_cols(win, 0, 512)

            def pre_qg(e):
                P.I("act", "activation", out=ex[e % 2], in_=Bc[:, e, :], func=AF.Exp, scale=-1.0 / 16)

            def ev_qg(e, ti, t0, tn, ps):
                P.I("dve", "scalar_tensor_tensor", out=qi[:, e, t0:t0 + tn], in0=ps, scalar=float(128 ** -0.5),
                    in1=ex[e % 2][:, t0:t0 + tn], op0=ALU.mult, op1=ALU.mult)
            proj_fm(w, 512, ev_qg, pre=pre_qg)
            w = load_w_cols(win, 512, 512)

            def pre_kg(e):
                ebi = ex[e % 2]
                ebe = ex[2 + e % 2]
                P.I("act", "activation", out=ebi, in_=Bc[:, e, :], func=AF.Exp, scale=1.0 / 16)
                P.I("dve", "tensor_tensor", out=ebe[:, 0:T].rearrange("p (c k) -> p c k", k=128),
                    in0=Bc[:, e, 0:T].rearrange("p (c k) -> p c k", k=128)[:, :, 127:128].to_broadcast([128, 8, 128]),
                    in1=Bc[:, e, 0:T].rearrange("p (c k) -> p c k", k=128), op=ALU.subtract)
                P.I("dve", "tensor_tensor", out=ebe[:, T:TT], in0=Bc[:, e, TT - 1:TT].to_broadcast([128, S]),
                    in1=Bc[:, e, T:TT], op=ALU.subtract)
                P.I("act", "activation", out=ebe, in_=ebe, func=AF.Exp, scale=-1.0 / 16)

            def ev_kg(e, ti, t0, tn, ps):
                P.I("dve", "tensor_tensor", out=ki[:, e, t0:t0 + tn], in0=ps, in1=ex[e % 2][:, t0:t0 + tn], op=ALU.mult)
                P.I("dve", "tensor_tensor", out=keT[:, e, t0:t0 + tn], in0=ps, in1=ex[2 + e % 2][:, t0:t0 + tn], op=ALU.mult)
            proj_fm(w, 512, ev_kg, pre=pre_kg)
            for bi, (b0, bn) in enumerate(BLKS):
                pb = bank_bf(bank())
                for h in range(4):
                    P.tr(pb[0:bn, h * 128:(h + 1) * 128], keT[:, h, b0:b0 + bn], ident_b)
                P.I("act", "activation", out=ke_tok[0:bn, bi, :], in_=pb[0:bn, 0:512], func=AF.Copy)
            A.release(mG)
            Vg = A.alloc(BF16, [9, 1024])
            for half in range(2):
                w = load_w_cols(win, 1024 + half * 512, 512)

                def ev_vg(bi, b0, bn, ps, half=half):
                    P.I("act", "activation", out=Vg[0:bn, bi, half * 512:(half + 1) * 512], in_=ps, func=AF.Copy)
                proj_tm(w, 512, ev_vg)

            chk(3)
            def gla_block(bi, Sst, emit_out, ogT):
                b0, bn = BLKS[bi]
                if emit_out:
                    for h in range(4):
                        pa = bank()
                        P.mm(pa[0:bn, 0:bn], ki[:, h, b0:b0 + bn], qi[:, h, b0:b0 + bn], True, True)
                        P.I("dve", "tensor_tensor", out=gla_att[h][0:bn, 0:bn], in0=pa[0:bn, 0:bn],
                            in1=gmask[0:bn, 0:bn], op=ALU.mult)
                for h in range(4):
                    if emit_out:
                        attm = gla_att[h]
                        po = [bank(), bank()]
                        sbt = Sb[bi % 2]
                        P.I("act", "activation", out=sbt[:, h, :], in_=Sst[:, h, :], func=AF.Copy)
                        for vc in range(2):
                            P.mm(po[vc][:, 0:bn], Vg[0:bn, bi, h * 256 + vc * 128:h * 256 + (vc + 1) * 128],
                                 attm[0:bn, 0:bn], True, False)
                        for vc in range(2):
                            P.mm(po[vc][:, 0:bn], sbt[:, h, vc * 128:(vc + 1) * 128],
                                 qi[:, h, b0:b0 + bn], False, True)
                    pk = bank()
                    P.mm(pk[:, 0:256], ke_tok[0:bn, bi, h * 128:(h + 1) * 128],
                         Vg[0:bn, bi, h * 256:(h + 1) * 256], True, True)
                    P.I("dve", "scalar_tensor_tensor", out=Sst[:, h, :], in0=Sst[:, h, :],
                        scalar=decay[:, h, bi:bi + 1], in1=pk[:, 0:256], op0=ALU.mult, op1=ALU.add)
                    if emit_out:
                        for vc in range(2):
                            P.I("act", "activation", out=ogT[:, h * 2 + vc, b0:b0 + bn], in_=po[vc][:, 0:bn],
                                func=AF.Copy)

            P.I("dve", "memset", ap=S32, constant=0.0)
            for bi in range(8):
                gla_block(bi, S32, False, None)
            P.dma("sp", cin_f[l][:, 0:1024].rearrange("p (h v) -> p h v", h=4), S32, dwrites=[("cin_f", l, 0)])
            P.cc([cin_f[l]], [cout_f[l]], GROUPS, dreads=[("cin_f", l, 0), ("cin_f", l, 1)], dwrites=[("cout_f", l)])
            for half in range(2):
                w = load_w_cols(win, 3088 + half * 512, 512)

                def ev_q(e, ti, t0, tn, ps, half=half):
                    h = half * 4 + e
                    if ti == 0:
                        bctr[0] += 1
                    st = bstage[bctr[0] % 2]
                    P.I("act", "activation", out=st[:, t0:t0 + tn], in_=ps, func=AF.Copy, scale=float(128 ** -0.5))
                    if ti == 2:
                        P.dma("sp", qscr[:, h, :], st, dwrites=[("qscr", h)])
                proj_fm(w, 512, ev_q)

            for half in range(2):
                w = load_w_cols(win, 2064 + half * 512, 512)

                def ev_rg(e, ti, t0, tn, ps, half=half):
                    c = half * 4 + e
                    if ti == 0:
                        bctr[0] += 1
                    st = bstage[bctr[0] % 2]
                    P.I("act", "activation", out=st[:, t0:t0 + tn], in_=ps, func=AF.Silu)
                    if ti == 2:
                        P.dma("sp", gscr[:, c, :], st, dwrites=[("gscr", c)])
                proj_fm(w, 512, ev_rg)

            P.dma("sp", S32s, sg[l].rearrange("h k v -> k h v"))
            mO = A.mark()
            ogT = A.alloc(F32, [8, TT])
            gla_att = [A.alloc(BF16, [128]) for _ in range(4)]
            gla_block(8, S32s, True, ogT)
            P.dma("sp", sso[l].rearrange("h k v -> k h v"), S32s)
            P.dma("sp", S32, cout_f[l][0:128, 0:1024].rearrange("p (h v) -> p h v", h=4), dreads=[("cout_f", l)])
            P.I("dve", "tensor_scalar", out=S32, in0=S32, scalar1=flg[:, 0:1], scalar2=None, op0=ALU.mult)
            for bi in range(8):
                gla_block(bi, S32, True, ogT)
            P.dma("sp", spo[l].rearrange("h k v -> k h v"), S32)
            osq = A.alloc(F32, [2, 512])
            orstd = A.alloc(F32, [512])
            gt2 = [A.alloc(BF16, [2, TT]) for _ in range(2)]
            for h in range(4):
                gg = gt2[h % 2]
                P.dma("sp", gg, gscr[:, 2 * h:2 * h + 2, :], dreads=[("gscr", 2 * h), ("gscr", 2 * h + 1)])
                for ti, (t0, tn) in enumerate(TILES):
                    P.I("act", "activation", out=osq[:, :, 0:tn], in_=ogT[:, 2 * h:2 * h + 2, t0:t0 + tn], func=AF.Square)
                    rms_rstd(lambda c: osq[:, c, 0:tn], tn, onesV, 2, orstd[:, 0:tn])
                    for vc in range(2):
                        c = 2 * h + vc
                        P.I("dve", "tensor_tensor", out=osq[:, vc, 0:tn], in0=ogT[:, c, t0:t0 + tn],
                            in1=orstd[:, 0:tn], op=ALU.mult)
                        P.I("dve", "scalar_tensor_tensor", out=act[:, c, t0:t0 + tn], in0=osq[:, vc, 0:tn],
                            scalar=g_on[:, l, c:c + 1], in1=gg[:, vc, t0:t0 + tn], op0=ALU.mult, op1=ALU.mult)
            A.release(mO)
            A.top = ll_top[0]

            chk(4)
            mF = A.mark()
            Rp = A.alloc(F32, [8, 8])
            P.dma("sp", Rp, cout_f[l][0:128, 1024:1088].rearrange("p (b h) -> p b h", b=8), dreads=[("cout_f", l)])
            P.I("dve", "tensor_scalar", out=Rp, in0=Rp, scalar1=negbig[:, 0:1], scalar2=None, op0=ALU.add)
            CL = A.alloc(F32, [16, 8])
            P.dma("sp", CL, clf[l].rearrange("(b p) h -> p b h", p=128))
            Rc = A.alloc(F32, [16, 8])
            Wc = A.alloc(F32, [16, 8])
            Tot = A.alloc(F32, [16, 8])
            Suf = A.alloc(F32, [16, 8])
            pb = bank()
            tri = A.alloc(F32, [128])
            P.I("dve", "tensor_scalar", out=tri, in0=dmask, scalar1=1.0 / (-NEG), scalar2=1.0, op0=ALU.mult, op1=ALU.add)
            P.mm(pb[:, 0:128], tri, CL.rearrange("p b h -> p (b h)"), True, True)
            P.I("act", "activation", out=Wc.rearrange("p b h -> p (b h)"), in_=pb[:, 0:128], func=AF.Copy)
            pb2 = bank()
            P.mm(pb2[:, 0:128], ones_f, CL.rearrange("p b h -> p (b h)"), True, True)
            P.I("act", "activation", out=Tot.rearrange("p b h -> p (b h)"), in_=pb2[:, 0:128], func=AF.Copy)
            P.I("dve", "memset", ap=Suf[:, 15, :], constant=0.0)
            for b in range(14, -1, -1):
                P.I("dve", "tensor_tensor", out=Suf[:, b, :], in0=Suf[:, b + 1, :], in1=Tot[:, b + 1, :], op=ALU.add)
            P.I("dve", "tensor_tensor", out=Rc, in0=Tot, in1=Wc, op=ALU.subtract)
            P.I("dve", "tensor_tensor", out=Rc, in0=Rc, in1=Suf, op=ALU.add)

            Bq = [A.alloc(F32, [TT]) for _ in range(2)]
            QTh = [A.alloc(BF16, [TT]) for _ in range(2)]
            selh = [A.alloc(F32, [128]) for _ in range(2)]
            KTo = [A.alloc(BF16, [T]) for _ in range(2)]
            Vo = [A.alloc(BF16, [8, 128]) for _ in range(2)]
            KTp = [A.alloc(BF16, [T]) for _ in range(2)]
            Vp = [A.alloc(BF16, [8, 128]) for _ in range(2)]
            Kc = [A.alloc(BF16, [16, 128]) for _ in range(2)]
            KTc = [A.alloc(BF16, [2048]) for _ in range(2)]
            Vc = [A.alloc(BF16, [16, 128]) for _ in range(2)]
            tmpS = [A.alloc(F32, [512]) for _ in range(7)]
            Pt = [A.alloc(BF16, [512]) for _ in range(7)]
            rden = A.alloc(F32, [512])
            kctr = [0]
            hcur = [0]

            NB = 7
            SKEW = 5

            def fox_S(lhsK, Vl, biascol, qcols, nq, pO, pD, first, last, diag, kn=128):
                i = kctr[0] % NB
                kctr[0] += 1
                h = hcur[0]
                pS = bank6()
                P.mm(pS[0:kn, 0:nq], lhsK, QTh[h % 2][:, qcols:qcols + nq], True, True)
                tS = tmpS[i]
                P.I("dve", "tensor_tensor", out=tS[0:kn, 0:nq], in0=pS[0:kn, 0:nq],
                    in1=Bq[h % 2][0:kn, qcols:qcols + nq], op=ALU.add)
                if diag:
                    dn = min(kn, nq)
                    P.I("dve", "tensor_tensor", out=tS[0:kn, 0:dn], in0=tS[0:kn, 0:dn], in1=dmask[0:kn, 0:dn], op=ALU.add)
                pt = Pt[i]
                P.I("act", "activation", out=pt[0:kn, 0:nq], in_=tS[0:kn, 0:nq], func=AF.Exp, bias=biascol, scale=1.0)
                return (pO, pD, Vl, pt[0:kn, 0:nq], first, last, kn)

            def fox_PV(rec):
                pO, pD, Vl, pt, first, last, kn = rec
                P.mm(pO, Vl, pt, first, last)
                P.mm(pD, ones_b[0:kn, :], pt, first, last)

            def fox_run(blocks):
                recs = []
                n = len(blocks)
                for i in range(n + SKEW):
                    if i < n:
                        recs.append(fox_S(*blocks[i][0], **blocks[i][1]))
                    if i >= SKEW:
                        fox_PV(recs[i - SKEW])

            coutV = cout_v[l][0:128, :].rearrange("p (b e) -> p b e", b=8)
            for h in range(8):
                hcur[0] = h
                bq = Bq[h % 2]
                sh = selh[h % 2]
                P.I("dve", "tensor_copy", out=sh[0:8, :], in_=ident_f[0:8, h:h + 1].to_broadcast([8, 128]))
                P.dma("sp", QTh[h % 2], qscr[:, h, :], dreads=[("qscr", h)])
                for ti, (t0, tn) in enumerate(TILES):
                    pb = bank()
                    P.mm(pb[:, 0:tn], sh[0:8, :], cT[0:8, t0:t0 + tn], True, True)
                    P.I("act", "activation", out=bq[:, t0:t0 + tn], in_=pb[:, 0:tn], func=AF.Copy)
                kto = KTo[h % 2]; vo = Vo[h % 2]
                ktp = KTp[h % 2]; vpp = Vp[h % 2]; kc = Kc[h % 2]; ktc = KTc[h % 2]; vc_ = Vc[h % 2]
                P.dma("sp", kto, cin_k[l][:, h * 1024:(h + 1) * 1024], dreads=[("cin_kv", l, "k", h)])
                P.dma("sp", vo, cinV[:, :, h * 128:(h + 1) * 128],
                      dreads=[("cin_kv", l, "v", b, h // 4) for b in range(8)])
                P.dma("sp", ktp, cout_k[l][0:128, h * 1024:(h + 1) * 1024], dreads=[("cout_k", l)])
                P.dma("sp", vpp, coutV[:, :, h * 128:(h + 1) * 128], dreads=[("cout_v", l)])
                for q4 in range(4):
                    P.dma("pool", kc[:, q4 * 4:(q4 + 1) * 4, :],
                          ck[l].rearrange("(b p) e -> p b e", p=128)[:, q4 * 4:(q4 + 1) * 4, h * 128:(h + 1) * 128])
                    P.dma("pool", vc_[:, q4 * 4:(q4 + 1) * 4, :],
                          cv[l].rearrange("(b p) e -> p b e", p=128)[:, q4 * 4:(q4 + 1) * 4, h * 128:(h + 1) * 128])
                for g4 in range(4):
                    pbb = bank_bf(bank())
                    for k in range(4):
                        P.tr(pbb[:, k * 128:(k + 1) * 128], kc[:, g4 * 4 + k, :], ident_b)
                    P.I("act", "activation", out=ktc[:, g4 * 512:(g4 + 1) * 512], in_=pbb[:, 0:512], func=AF.Copy)
                for qt in range(2):
                    q0 = qt * 512
                    pO = pst[:, 6, :]; pD = pst[:, 7, :]
                    nown = 4 * qt + 4
                    blocks = []
                    for kb in range(8):
                        blocks.append(((ktp[:, kb * 128:(kb + 1) * 128], vpp[:, kb, :], Rp[:, kb, h:h + 1],
                                        q0, 512, pO[:, 0:512], pD[:, 0:512], kb == 0, False, False), {}))
                    for kb in range(nown):
                        lo = max(q0, kb * 128)
                        nq = q0 + 512 - lo
                        blocks.append(((kto[:, kb * 128:(kb + 1) * 128], vo[:, kb, :],
                                        negc[:, kb, h:h + 1], lo, nq, pO[:, lo - q0:512], pD[:, lo - q0:512],
                                        False, kb == nown - 1, kb * 128 >= q0), {}))
                    fox_run(blocks)
                    P.I("dve", "reciprocal", out=rden, in_=pD[:, 0:512])
                    P.I("dve", "tensor_tensor", out=act[:, 8 + h, q0:q0 + 512], in0=pO[:, 0:512], in1=rden, op=ALU.mult)
                pO = pst[:, 6, :]; pD = pst[:, 7, :]
                blocks = []
                for kb in range(16):
                    blocks.append(((ktc[:, kb * 128:(kb + 1) * 128], vc_[:, kb, :], Rc[:, kb, h:h + 1],
                                    T, S, pO[:, 0:S], pD[:, 0:S], kb == 0, False, False), {}))
                blocks.append(((KTs[:, h, :], Vs[0:S, h * 128:(h + 1) * 128], negc[0:S, 8, h:h + 1],
                                T, S, pO[:, 0:S], pD[:, 0:S], False, True, True), {"kn": S}))
                fox_run(blocks)
                P.I("dve", "reciprocal", out=rden[:, 0:S], in_=pD[:, 0:S])
                P.I("dve", "tensor_tensor", out=act[:, 8 + h, T:TT], in0=pO[:, 0:S], in1=rden[:, 0:S], op=ALU.mult)
            A.release(mF)
            A.release(mL)

            chk(5)
            mW = A.mark()
            yT = A.alloc(F32, [16, TT])
            wo = w_out[l]
            for q4 in range(4):
                w = load_w_cols(wo, q4 * 512, 512)

                def ev_o(e, ti, t0, tn, ps, q4=q4):
                    P.I("act", "activation", out=yT[:, q4 * 4 + e, t0:t0 + tn], in_=ps, func=AF.Copy)
                proj_fm(w, 512, ev_o)
            postnorm_residual(yT, g_post, l, False, nxt=(g_mlp, l))
            chk(6)

            mU = A.mark()
            u = [A.alloc(BF16, [4, TT]) for _ in range(2)]
            rtmp = [A.alloc(F32, [512]) for _ in range(2)]
            rctr = 0
            for f in range(16):
                wu = load_w_cols(w_up[l], f * 512, 512)
                wd = load_w_rows(w_dn[l], f * 512, 4, 2048)
                uf = u[f % 2]

                def ev_up(e, ti, t0, tn, ps, uf=uf):
                    nonlocal rctr
                    r = rtmp[rctr % 2]
                    rctr += 1
                    P.I("act", "activation", out=r[:, 0:tn], in_=ps, func=AF.Relu)
                    P.I("dve", "tensor_tensor", out=uf[:, e, t0:t0 + tn], in0=r[:, 0:tn], in1=r[:, 0:tn], op=ALU.mult)
                proj_fm(wu, 512, ev_up)
                for dc in range(16):
                    banks = [bank() for _ in MTILES]
                    for e in range(4):
                        for ti, (t0, tn) in enumerate(MTILES):
                            P.mm(banks[ti][:, 0:tn], wd[:, e, dc * 128:(dc + 1) * 128], uf[:, e, t0:t0 + tn], e == 0, e == 3)
                    for ti, (t0, tn) in enumerate(MTILES):
                        if f == 0:
                            P.I("act", "activation", out=yT[:, dc, t0:t0 + tn], in_=banks[ti][:, 0:tn], func=AF.Copy)
                        else:
                            P.I("dve", "tensor_tensor", out=yT[:, dc, t0:t0 + tn], in0=yT[:, dc, t0:t0 + tn],
                                in1=banks[ti][:, 0:tn], op=ALU.add)
            A.release(mU)
            postnorm_residual(yT, g_mpost, l, final_layer, nxt=None if final_layer else (g_pre, l + 1))
            A.release(mW)

    try:
        layers()
    except _Stop:
        pass
    P.emit()
    es.close()
    print("SBUF arena peak bytes", A.peak, "ops", len(P.ops))
    return nc


_NC_CACHE = {}


def kernel(x_prompt, x_sample, cache_fox_k, cache_fox_v, cache_fox_logf, state_gla,
           g_mix_pre, w_in, w_gla_gate_up, b_gla_gate, b_fox_f, g_gla_onorm, w_out,
           g_mix_post, g_mlp_pre, w_mlp_up, w_mlp_down, g_mlp_post):
    f = lambda a: np.ascontiguousarray(np.asarray(a, dtype=np.float32))
    if "nc" not in _NC_CACHE:
        _NC_CACHE["nc"] = build()
    nc = _NC_CACHE["nc"]
    cst = make_consts()
    shared = dict(cst=cst, gmp=f(g_mix_pre), w_in=f(w_in), w2=f(w_gla_gate_up), bg=f(b_gla_gate),
                  bfx=f(b_fox_f), gon=f(g_gla_onorm), w_out=f(w_out), gpost=f(g_mix_post), gmlp=f(g_mlp_pre),
                  w_up=f(w_mlp_up), w_dn=f(w_mlp_down), gmpost=f(g_mlp_post))
    x_prompt = np.asarray(x_prompt); x_sample = np.asarray(x_sample)
    cache_fox_k = np.asarray(cache_fox_k); cache_fox_v = np.asarray(cache_fox_v)
    cache_fox_logf = np.asarray(cache_fox_logf); state_gla = np.asarray(state_gla)
    in_maps = []
    for c in range(8):
        b, half = c // 2, c % 2
        m = dict(shared)
        m["xp"] = f(x_prompt[b, half * T:(half + 1) * T])
        m["xs"] = f(x_sample[c])
        m["ck"] = f(cache_fox_k[:, c].reshape(NL, 2048, 1024))
        m["cv"] = f(cache_fox_v[:, c].reshape(NL, 2048, 1024))
        m["clf"] = f(cache_fox_logf[:, c])
        m["sg"] = f(state_gla[:, c])
        m["flag"] = np.full((128, 1), float(half), np.float32)
        in_maps.append(m)
    res = run_bass_kernel_spmd(nc, in_maps, core_ids=list(range(8)))
    R = res.results
    y_p = np.zeros((4, 2048, D), np.float32)
    k_p = np.zeros((NL, 4, 2048, 8, 128), np.float32)
    v_p = np.zeros((NL, 4, 2048, 8, 128), np.float32)
    f_p = np.zeros((NL, 4, 2048, 8), np.float32)
    s_p = np.zeros((NL, 4, 4, 128, 256), np.float32)
    y_s = np.zeros((8, S, D), np.float32)
    k_s = np.zeros((NL, 8, S, 8, 128), np.float32)
    v_s = np.zeros((NL, 8, S, 8, 128), np.float32)
    f_s = np.zeros((NL, 8, S, 8), np.float32)
    s_s = np.zeros((NL, 8, 4, 128, 256), np.float32)
    for c in range(8):
        b, half = c // 2, c % 2
        r = R[c]
        sl = slice(half * T, (half + 1) * T)
        y_p[b, sl] = r["yp"]
        k_p[:, b, sl] = r["kp"].reshape(NL, T, 8, 128)
        v_p[:, b, sl] = r["vp"].reshape(NL, T, 8, 128)
        f_p[:, b, sl] = r["fp"]
        if half == 1:
            s_p[:, b] = r["spo"]
        y_s[c] = r["ys"]
        k_s[:, c] = r["kso"].reshape(NL, S, 8, 128)
        v_s[:, c] = r["vso"].reshape(NL, S, 8, 128)
        f_s[:, c] = r["fso"]
        s_s[:, c] = r["sso"]
    return (y_p, y_s, k_p, v_p, f_p, s_p, k_s, v_s, f_s, s_s)
```
